# Optimizing a Trainium2 kernel written in Bass

```python
import math
import jax
import jax.numpy as jnp
from jax import lax
import numpy as np

D_MODEL = 4096
BATCH = 8
SEQ = 2048
DEPTH = 4

N_HEADS = 32
HEAD_DIM = D_MODEL // N_HEADS
N_KV_GROUPS = 4
GROUP_SIZE = N_HEADS // N_KV_GROUPS
N_A_LAYERS = DEPTH // 2
CMP_BLOCK = 32
CMP_STRIDE = 16
CMP_HIDDEN = HEAD_DIM
SLC_BLOCK = 64
N_SELECT = 16
WINDOW = 512
Q_BLOCK = 128
N_BUCKETS = 32
MAX_DISTANCE = 128
RMS_EPS = 1e-6
FORCE_SCORE = 1e6
NSA_IN = 4 * N_HEADS * HEAD_DIM + 6 * N_KV_GROUPS * HEAD_DIM + 3 * N_HEADS
SB_IN = 2 * N_HEADS * HEAD_DIM

kernel_name = 'hybrid_nsa_stickbreaking_yoco'


def rms_norm(x, g):
    xf = x.astype(jnp.float32)
    y = xf * lax.rsqrt(jnp.mean(xf * xf, axis=-1, keepdims=True) + RMS_EPS)
    return (y * g.astype(jnp.float32)).astype(x.dtype)


def t5_bucket(dist):
    max_exact = N_BUCKETS // 2
    d = jnp.maximum(dist, 0)
    log_ratio = jnp.log(jnp.maximum(d, max_exact).astype(jnp.float32) / max_exact)
    large = max_exact + (log_ratio / math.log(MAX_DISTANCE / max_exact)
                         * (N_BUCKETS - max_exact)).astype(jnp.int32)
    return jnp.where(d < max_exact, d, jnp.minimum(large, N_BUCKETS - 1))


def static_bias(rel_bias, dist):
    q_len, k_len = dist.shape
    b = rel_bias[t5_bucket(dist)].astype(jnp.float32)
    return b.transpose(2, 0, 1).reshape(N_KV_GROUPS, GROUP_SIZE, q_len, k_len)


def masked_softmax(s, mask):
    s = jnp.where(mask, s, -jnp.inf)
    m = jnp.max(s, axis=-1, keepdims=True)
    m = jnp.where(jnp.isfinite(m), m, 0.0)
    e = jnp.where(mask, jnp.exp(s - m), 0.0)
    return e / jnp.maximum(jnp.sum(e, axis=-1, keepdims=True), 1e-30)


def compress(k, pos, w1, w2):
    b, g, t, dh = k.shape
    n_cmp = (t - CMP_BLOCK) // CMP_STRIDE + 1
    idx = np.arange(n_cmp)[:, None] * CMP_STRIDE + np.arange(CMP_BLOCK)[None, :]
    blocks = (k[:, :, idx, :] + pos).reshape(b, g, n_cmp, CMP_BLOCK * dh)
    return jax.nn.silu(blocks @ w1) @ w2


def nsa_layer(h, w_in, cmp_pos, cmp_k_w1, cmp_k_w2, cmp_v_w1, cmp_v_w2, w_out, rel_bias):
    B, T, _ = h.shape
    G, R, Dh = N_KV_GROUPS, GROUP_SIZE, HEAD_DIM
    HD, GD = N_HEADS * HEAD_DIM, N_KV_GROUPS * HEAD_DIM
    scale = 1.0 / math.sqrt(Dh)
    cuts = [int(c) for c in np.cumsum([HD, GD, GD, GD, GD, GD, GD, 3 * N_HEADS, HD, HD])]
    (q, kc_raw, vc_raw, ks_raw, vs_raw, kw_raw, vw_raw, gate_logits,
     z_c, z_s, z_w) = jnp.split(h @ w_in, cuts, axis=-1)

    def to_groups(t):
        return t.reshape(B, T, G, Dh).transpose(0, 2, 1, 3)

    q = q.reshape(B, T, G, R, Dh).transpose(0, 2, 3, 1, 4)

    n_cmp = (T - CMP_BLOCK) // CMP_STRIDE + 1
    cmp_start = (np.arange(n_cmp) * CMP_STRIDE).astype(np.int32)
    cmp_end = (cmp_start + CMP_BLOCK - 1).astype(np.int32)
    kc = compress(to_groups(kc_raw), cmp_pos, cmp_k_w1, cmp_k_w2)
    vc = compress(to_groups(vc_raw), cmp_pos, cmp_v_w1, cmp_v_w2)

    n_slc = T // SLC_BLOCK
    n_sel = min(N_SELECT, n_slc)
    slc_start = (np.arange(n_slc) * SLC_BLOCK).astype(np.int32)
    overlap = jnp.asarray(((cmp_start[:, None] < slc_start[None, :] + SLC_BLOCK)
                           & (cmp_end[:, None] >= slc_start[None, :])).astype(np.float32))
    ks_blk = to_groups(ks_raw).reshape(B, G, n_slc, SLC_BLOCK, Dh)
    vs_blk = to_groups(vs_raw).reshape(B, G, n_slc, SLC_BLOCK, Dh)
    b_idx = jnp.arange(B)[:, None, None, None]
    g_idx = jnp.arange(G)[None, :, None, None]
    bias_by_group = rel_bias.reshape(N_BUCKETS, G, R).transpose(1, 0, 2)

    pad = ((0, 0), (0, 0), (WINDOW, 0), (0, 0))
    kw = jnp.pad(to_groups(kw_raw), pad)
    vw = jnp.pad(to_groups(vw_raw), pad)

    def query_block(i):
        q0 = i * Q_BLOCK
        qi = lax.dynamic_slice_in_dim(q, q0, Q_BLOCK, axis=3)
        tpos = q0 + jnp.arange(Q_BLOCK)

        dist_c = tpos[:, None] - cmp_end[None, :]
        s_c = (jnp.einsum('bgrqd,bgnd->bgrqn', qi, kc, preferred_element_type=jnp.float32) * scale
               + static_bias(rel_bias, dist_c))
        p_c = masked_softmax(s_c, dist_c >= 0)
        o_c = jnp.einsum('bgrqn,bgnd->bgrqd', p_c.astype(vc.dtype), vc)

        imp = jnp.einsum('bgrqn,ns->bgqs', p_c, overlap)
        j = jnp.arange(n_slc)[None, :]
        cur = (tpos // SLC_BLOCK)[:, None]
        forced = (j == 0) | (j == cur) | (j == cur - 1)
        valid = slc_start[None, :] <= tpos[:, None]
        imp = jnp.where(forced, FORCE_SCORE, jnp.where(valid, imp, -1.0))
        _, sel = lax.top_k(imp, n_sel)
        n_key = n_sel * SLC_BLOCK
        ks = ks_blk[b_idx, g_idx, sel].reshape(B, G, Q_BLOCK, n_key, Dh)
        vs = vs_blk[b_idx, g_idx, sel].reshape(B, G, Q_BLOCK, n_key, Dh)
        kpos_s = (sel[..., None] * SLC_BLOCK + jnp.arange(SLC_BLOCK)).reshape(B, G, Q_BLOCK, n_key)
        dist_s = tpos[None, None, :, None] - kpos_s
        bias_s = jnp.moveaxis(bias_by_group[g_idx, t5_bucket(dist_s)], -1, 2).astype(jnp.float32)
        s_s = (jnp.einsum('bgrqd,bgqkd->bgrqk', qi, ks, preferred_element_type=jnp.float32) * scale
               + bias_s)
        p_s = masked_softmax(s_s, (dist_s >= 0)[:, :, None])
        o_s = jnp.einsum('bgrqk,bgqkd->bgrqd', p_s.astype(vs.dtype), vs)

        kwi = lax.dynamic_slice_in_dim(kw, q0, Q_BLOCK + WINDOW, axis=2)
        vwi = lax.dynamic_slice_in_dim(vw, q0, Q_BLOCK + WINDOW, axis=2)
        kpos_w = q0 - WINDOW + jnp.arange(Q_BLOCK + WINDOW)
        dist_w = tpos[:, None] - kpos_w[None, :]
        mask_w = (dist_w >= 0) & (dist_w < WINDOW) & (kpos_w[None, :] >= 0)
        s_w = (jnp.einsum('bgrqd,bgkd->bgrqk', qi, kwi, preferred_element_type=jnp.float32) * scale
               + static_bias(rel_bias, dist_w))
        p_w = masked_softmax(s_w, mask_w)
        o_w = jnp.einsum('bgrqk,bgkd->bgrqd', p_w.astype(vwi.dtype), vwi)
        return o_c, o_s, o_w

    o_c, o_s, o_w = lax.map(query_block, jnp.arange(T // Q_BLOCK))

    def gated(o, z):
        o = o.transpose(1, 0, 4, 2, 3, 5).reshape(B, T, N_HEADS, Dh)
        return o * jax.nn.silu(z.reshape(B, T, N_HEADS, Dh))

    gates = jax.nn.sigmoid(gate_logits).reshape(B, T, 3, N_HEADS, 1)
    mixed = (gates[:, :, 0] * gated(o_c, z_c) + gates[:, :, 1] * gated(o_s, z_s)
             + gates[:, :, 2] * gated(o_w, z_w))
    return mixed.reshape(B, T, HD) @ w_out


def stick_breaking_layer(h, w_in, w_out, k_sh, v_sh):
    B, T, _ = h.shape
    H, Dh = N_HEADS, HEAD_DIM
    scale = 1.0 / math.sqrt(Dh)
    q, z = jnp.split(h @ w_in, 2, axis=-1)
    q = q.reshape(B, T, H, Dh).transpose(0, 2, 1, 3)
    kpos = jnp.arange(T)

    def query_block(i):
        q0 = i * Q_BLOCK
        qi = lax.dynamic_slice_in_dim(q, q0, Q_BLOCK, axis=2)
        tpos = q0 + jnp.arange(Q_BLOCK)
        logits = jnp.einsum('bhqd,bhkd->bhqk', qi, k_sh, preferred_element_type=jnp.float32) * scale
        mask = kpos[None, :] < tpos[:, None]
        neg_log_keep = jnp.where(mask, jax.nn.softplus(logits), 0.0)
        between = lax.cumsum(neg_log_keep, axis=3, reverse=True) - neg_log_keep
        a = jnp.where(mask, jnp.exp(jax.nn.log_sigmoid(logits) - between), 0.0)
        return jnp.einsum('bhqk,bhkd->bhqd', a.astype(v_sh.dtype), v_sh)

    o = lax.map(query_block, jnp.arange(T // Q_BLOCK))
    o = o.transpose(1, 0, 3, 2, 4).reshape(B, T, H * Dh)
    return (o * jax.nn.silu(z)) @ w_out


def setup_inputs(seed: int = 0) -> dict:
    key = jax.random.key(seed)
    keys = iter(jax.random.split(key, 48))
    HD = N_HEADS * HEAD_DIM

    def normal(shape, scale):
        return jax.random.normal(next(keys), shape, jnp.float32) * scale

    def gain():
        return 1.0 + normal((D_MODEL,), 0.01)

    inputs = {'x': normal((BATCH, SEQ, D_MODEL), 1.0),
              'rel_bias': normal((N_BUCKETS, N_HEADS), 0.5)}
    for l in range(N_A_LAYERS):
        p = f'a{l}_'
        inputs[p + 'norm'] = gain()
        inputs[p + 'w_in'] = normal((D_MODEL, NSA_IN), D_MODEL ** -0.5)
        inputs[p + 'cmp_pos'] = normal((CMP_BLOCK, HEAD_DIM), 0.1)
        inputs[p + 'cmp_k_w1'] = normal((CMP_BLOCK * HEAD_DIM, CMP_HIDDEN), (CMP_BLOCK * HEAD_DIM) ** -0.5)
        inputs[p + 'cmp_k_w2'] = normal((CMP_HIDDEN, HEAD_DIM), CMP_HIDDEN ** -0.5)
        inputs[p + 'cmp_v_w1'] = normal((CMP_BLOCK * HEAD_DIM, CMP_HIDDEN), (CMP_BLOCK * HEAD_DIM) ** -0.5)
        inputs[p + 'cmp_v_w2'] = normal((CMP_HIDDEN, HEAD_DIM), CMP_HIDDEN ** -0.5)
        inputs[p + 'w_out'] = normal((HD, D_MODEL), HD ** -0.5)
    inputs['kv_norm'] = gain()
    inputs['w_kv'] = normal((D_MODEL, 2 * HD), D_MODEL ** -0.5)
    for l in range(N_A_LAYERS, DEPTH):
        p = f'b{l}_'
        inputs[p + 'norm'] = gain()
        inputs[p + 'w_in'] = normal((D_MODEL, SB_IN), D_MODEL ** -0.5)
        inputs[p + 'w_out'] = normal((HD, D_MODEL), HD ** -0.5)
    inputs['final_norm'] = gain()
    return inputs


def reference(x, rel_bias,
              a0_norm, a0_w_in, a0_cmp_pos, a0_cmp_k_w1, a0_cmp_k_w2, a0_cmp_v_w1, a0_cmp_v_w2, a0_w_out,
              a1_norm, a1_w_in, a1_cmp_pos, a1_cmp_k_w1, a1_cmp_k_w2, a1_cmp_v_w1, a1_cmp_v_w2, a1_w_out,
              kv_norm, w_kv,
              b2_norm, b2_w_in, b2_w_out,
              b3_norm, b3_w_in, b3_w_out,
              final_norm):
    nsa_params = [
        (a0_norm, a0_w_in, a0_cmp_pos, a0_cmp_k_w1, a0_cmp_k_w2, a0_cmp_v_w1, a0_cmp_v_w2, a0_w_out),
        (a1_norm, a1_w_in, a1_cmp_pos, a1_cmp_k_w1, a1_cmp_k_w2, a1_cmp_v_w1, a1_cmp_v_w2, a1_w_out),
    ]
    sb_params = [(b2_norm, b2_w_in, b2_w_out), (b3_norm, b3_w_in, b3_w_out)]
    B, T, _ = x.shape
    k_sh = v_sh = None
    for layer in range(DEPTH):
        if layer < N_A_LAYERS:
            norm, w_in, pos, kw1, kw2, vw1, vw2, w_out = nsa_params[layer]
            x = x + nsa_layer(rms_norm(x, norm), w_in, pos, kw1, kw2, vw1, vw2, w_out, rel_bias)
        else:
            if layer == N_A_LAYERS:
                k_flat, v_flat = jnp.split(rms_norm(x, kv_norm) @ w_kv, 2, axis=-1)
                k_sh = k_flat.reshape(B, T, N_HEADS, HEAD_DIM).transpose(0, 2, 1, 3)
                v_sh = v_flat.reshape(B, T, N_HEADS, HEAD_DIM).transpose(0, 2, 1, 3)
            norm, w_in, w_out = sb_params[layer - N_A_LAYERS]
            x = x + stick_breaking_layer(rms_norm(x, norm), w_in, w_out, k_sh, v_sh)
    return rms_norm(x, final_norm)
```

```python
import math
import os
from contextlib import ExitStack

import numpy as np
import concourse.bass as bass
import concourse.mybir as mybir
from concourse.bass_utils import run_bass_kernel_spmd

F32 = mybir.dt.float32
BF16 = mybir.dt.bfloat16
ALU = mybir.AluOpType
AF = mybir.ActivationFunctionType

T, D, H, G, R, DH = 2048, 4096, 32, 4, 8, 128
NT, NQ, KC = 16, 4, 32
NCMP, NSLC, NSEL = 127, 32, 16
NSA_IN = 19552
NEG = -30000.0
QSCALE = 1.0 / math.sqrt(DH)
BW = 256
SB_X = 1152
WB_X = 1408
MB_X = 896

ENGS = ['pe', 'act', 'dve', 'pool', 'sp']
NDS = 8


class Prog:
    def __init__(self, nc, stack):
        self.nc = nc
        self.ops = {e: [] for e in ENGS}
        self.cnt = {e: 0 for e in ENGS}
        self.pending = {e: False for e in ENGS}
        self.semh = {}
        for e in ['pe', 'act', 'dve', 'pool']:
            self.semh[('c', e)] = stack.enter_context(nc.semaphore(f"c_{e}"))
        self.dcnt = {}
        self.drr = {}
        for q in ['sp', 'act', 'pool']:
            self.drr[q] = 0
            for i in range(NDS):
                self.semh[('d', q, i)] = stack.enter_context(nc.semaphore(f"d_{q}{i}"))
                self.dcnt[(q, i)] = 0
        self.seen = {e: {} for e in ENGS}
        self.res = {}

    def _deps(self, reads, writes):
        deps = {}

        def add(k, v):
            if deps.get(k, 0) < v:
                deps[k] = v
        for r in reads:
            st = self.res.get(r)
            if st and st[0]:
                add(*st[0])
        for w in writes:
            st = self.res.get(w)
            if st:
                if st[0]:
                    add(*st[0])
                for k, v in st[1].items():
                    add(k, v)
        return deps

    def _commit(self, ev, reads, writes):
        k, v = ev
        for r in reads:
            st = self.res.setdefault(r, [None, {}])
            if st[1].get(k, 0) < v:
                st[1][k] = v
        for w in writes:
            self.res[w] = [ev, {}]

    def _waits(self, eng, deps):
        waits = []
        for k, v in deps.items():
            if eng == 'pe' and k == ('c', 'pe'):
                continue
            if self.seen[eng].get(k, 0) >= v:
                continue
            self.seen[eng][k] = v
            waits.append((k, v))
        return waits

    def op(self, eng, fn, reads=(), writes=(), inc=True):
        deps = self._deps(reads, writes)
        waits = self._waits(eng, deps)
        if inc:
            self.cnt[eng] += 1
            ev = (('c', eng), self.cnt[eng])
            self.pending[eng] = False
        else:
            ev = (('c', eng), self.cnt[eng] + 1)
            self.pending[eng] = True
        self.ops[eng].append((fn, waits, ev if inc else None))
        self._commit(ev, reads, writes)

    def dma(self, q, fn, reads=(), writes=()):
        deps = self._deps(reads, writes)
        i = self.drr[q]
        self.drr[q] = (i + 1) % NDS
        k = ('d', q, i)
        prev = self.dcnt[(q, i)]
        if prev > 0 and deps.get(k, 0) < prev:
            deps[k] = prev
        waits = self._waits(q, deps)
        self.dcnt[(q, i)] += 16
        ev = (k, self.dcnt[(q, i)])
        self.ops[q].append((fn, waits, ev))
        self._commit(ev, reads, writes)

    def _all_targets(self):
        t = []
        for (q, i), v in self.dcnt.items():
            if v > 0:
                t.append((('d', q, i), v))
        for e in ['pe', 'act', 'dve', 'pool']:
            if self.cnt[e] > 0:
                t.append((('c', e), self.cnt[e]))
        return t

    def barrier(self):
        for e in ENGS:
            assert not self.pending[e], e
        tg = self._all_targets()
        for e in ENGS:
            waits = []
            for k, v in tg:
                if e == 'pe' and k == ('c', 'pe'):
                    continue
                if self.seen[e].get(k, 0) >= v:
                    continue
                self.seen[e][k] = v
                waits.append((k, v))
            if waits:
                self.ops[e].append((None, waits, None))
        self.res.clear()

    def finish(self):
        for e in ENGS:
            assert not self.pending[e], e
        self.ops['sp'].append((None, self._all_targets(), None))

    def emit(self, block):
        def run(name, e):
            for fn, waits, ev in self.ops[name]:
                for k, v in waits:
                    e.wait_ge(self.semh[k], v)
                if fn is None:
                    continue
                ins = fn(e)
                if ev is not None:
                    k, v = ev
                    ins.then_inc(self.semh[k], 16 if k[0] == 'd' else 1)

        @block.tensor
        def _(e):
            run('pe', e)

        @block.scalar
        def _(e):
            run('act', e)

        @block.vector
        def _(e):
            run('dve', e)

        @block.gpsimd
        def _(e):
            run('pool', e)

        @block.sync
        def _(e):
            run('sp', e)


def MM(o, l, r, st, sp, skip=False):
    if skip:
        return lambda e: e.matmul(o, l, r, start=st, stop=sp, skip_group_check=True)
    return lambda e: e.matmul(o, l, r, start=st, stop=sp)


def TR(o, i, idn):
    return lambda e: e.transpose(o, i, idn)


def ACT(o, i, f, bias=None, scale=None, accum=None):
    kw = {}
    if bias is not None:
        kw['bias'] = bias
    if scale is not None:
        kw['scale'] = scale
    if accum is not None:
        kw['accum_out'] = accum
    return lambda e: e.activation(o, i, f, **kw)


def TS(o, i, s1, s2, op0, op1=None):
    if op1 is None:
        return lambda e: e.tensor_scalar(o, i, s1, None, op0)
    return lambda e: e.tensor_scalar(o, i, s1, s2, op0, op1)


def TT(o, a, b, op):
    return lambda e: e.tensor_tensor(o, a, b, op)


def STT(o, i0, s, i1, op0, op1):
    return lambda e: e.scalar_tensor_tensor(o, i0, s, i1, op0, op1)


def CP(o, i):
    return lambda e: e.tensor_copy(o, i)


def MSET(o, v):
    return lambda e: e.memset(o, v)


def DMA(o, i):
    return lambda e: e.dma_start(out=o, in_=i)


class Region:
    def __init__(self, ap):
        self.ap = ap
        self.off = 0
        self.cap = ap.shape[1] * 2

    def reset(self):
        self.off = 0

    def take(self, shape, dt):
        esz = 4 if dt == F32 else 2
        n = int(np.prod(shape[1:])) * esz
        assert self.off + n <= self.cap, (self.off, n, self.cap)
        v = self.ap[0:shape[0], self.off // 2:(self.off + n) // 2]
        self.off += (n + 31) // 32 * 32
        if dt == F32:
            v = v.bitcast(F32)
        if len(shape) == 3:
            v = v.rearrange("p (a b) -> p a b", b=shape[2])
        return v


def nsa_blocks():
    bl = []
    HD, GD = H * DH, G * DH
    for i in range(HD // BW):
        bl.append((i * BW, BW, 'B', 'q', 2 * i))
    c = HD
    for name, mode in [('kc', 'B'), ('vc', 'B'), ('ks', 'B'), ('vs', 'A'), ('kw', 'B'), ('vw', 'A')]:
        for i in range(GD // BW):
            bl.append((c + i * BW, BW, mode, name, 2 * i))
        c += GD
    bl.append((c, 96, 'A', 'gates', 0))
    c += 96
    for name in ['zc', 'zs', 'zw']:
        for i in range(HD // BW):
            bl.append((c + i * BW, BW, 'A', name, 2 * i))
        c += HD
    assert c == NSA_IN
    return bl


def sb_blocks(qname, zname):
    bl = []
    HD = H * DH
    for i in range(HD // BW):
        bl.append((i * BW, BW, 'B', qname, 2 * i))
    for i in range(HD // BW):
        bl.append((HD + i * BW, BW, 'A', zname, 2 * i))
    return bl


def out_blocks():
    return [(i * BW, BW, 'O', 'o', i) for i in range(D // BW)]


def tile_weight(w, blocks):
    out = np.zeros((len(blocks), 128, KC * BW), np.float32)
    o4 = out.reshape(len(blocks), 128, KC, BW)
    for bi, (c0, wd, _, _, _) in enumerate(blocks):
        o4[bi, :, :, :wd] = w[:, c0:c0 + wd].reshape(KC, 128, wd).transpose(1, 0, 2)
    return out


def t5_bucket_np(d):
    d = np.maximum(d, 0)
    lr = np.log(np.maximum(d, 16).astype(np.float32) / np.float32(16)).astype(np.float32)
    large = 16 + (lr / np.float32(math.log(128 / 16)) * np.float32(16)).astype(np.int32)
    return np.where(d < 16, d, np.minimum(large, 31))


def host_tables(rel_bias):
    rb = np.asarray(rel_bias, np.float32)
    p = np.arange(128)[:, None]
    def band(X, hi=None):
        x = np.arange(X)[None, :]
        delta = x - 384 - p
        b = rb[t5_bucket_np(delta)]
        ok = delta >= 0
        if hi is not None:
            ok = ok & (delta < hi)
        b = np.where(ok[:, :, None], b, np.float32(NEG))
        return np.ascontiguousarray(b.transpose(2, 0, 1)).astype(np.float32)
    sband = band(SB_X)
    wband = band(WB_X, 512)
    n = np.arange(NCMP)[:, None]
    t = np.arange(T)[None, :]
    dc = t - (16 * n + 31)
    cb = rb[t5_bucket_np(dc)]
    cb = np.where((dc >= 0)[:, :, None], cb, np.float32(NEG))
    cband = np.ascontiguousarray(cb.transpose(2, 0, 1)).astype(np.float32)
    b31 = np.ascontiguousarray(np.broadcast_to(rb[31][None, :], (128, H))).astype(np.float32)
    return sband, wband, cband, b31


def const_tables():
    p = np.arange(128)[:, None]
    x = np.arange(MB_X)[None, :]
    mband = ((x - 384 - p) > 0).astype(np.float32)
    tt = np.arange(NT)[None, :, None]
    pp = np.arange(128)[:, None, None]
    j = np.arange(NSLC)[None, None, :]
    tpos = tt * 128 + pp
    cur = tpos // 64
    forced = (j == 0) | (j == cur) | (j == cur - 1)
    valid = (j * 64) <= tpos
    topA = (valid & ~forced).astype(np.float32)
    topC = np.where(forced, np.float32(1e6), np.where(valid, np.float32(0.0), np.float32(-1.0))).astype(np.float32)
    E = np.zeros((NSLC, NT, 128), np.float32)
    for kt in range(NT):
        E[2 * kt, kt, :64] = 1.0
        E[2 * kt + 1, kt, 64:] = 1.0
    cs = np.arange(NCMP) * 16
    ce = cs + 31
    ss = np.arange(NSLC) * 64
    ov = ((cs[:, None] < ss[None, :] + 64) & (ce[:, None] >= ss[None, :])).astype(np.float32)
    return mband, np.ascontiguousarray(topA), np.ascontiguousarray(topC), E.reshape(NSLC, NT * 128), ov


class Ctx:
    def dump(self, name, ap, reads):
        d = self.nc.dram_tensor("dbg_" + name, list(ap.shape), ap.dtype, kind="ExternalOutput").ap()
        self.P.dma('sp', DMA(d, ap), reads=reads)


def build(dbg=False, stop_after=None, start_at=None, sub=None):
    nc = bass.Bass("TRN2", target_bir_lowering=False)
    okind = "ExternalOutput" if dbg else "Internal"

    def din(name, shape, dt=F32):
        return nc.dram_tensor(name, list(shape), dt, kind="ExternalInput").ap()

    def dscr(name, shape, dt):
        k = okind
        if start_at == 'sb' and name in ('featT', 'szH', 'kshT', 'vshH'):
            k = "ExternalInput"
        elif start_at == 'sb':
            pass
        elif start_at is not None and name in ('featT', 'vsH', 'gatesH', 'szH'):
            k = "ExternalInput"
        return nc.dram_tensor(name, list(shape), dt, kind=k).ap()

    c = Ctx()
    c.nc = nc
    x_in = din("x", [T, D])
    nb_nsa, nb_sb, nb_o = len(nsa_blocks()), len(sb_blocks('q', 'z')), len(out_blocks())
    if start_at is not None:
        nb_nsa = nb_sb = nb_o = 1
    c.sub = sub
    c.dumps = []
    wts = {}
    for l in range(2):
        p = f"a{l}_"
        wts[p + 'w_in'] = din(p + 'w_in', [nb_nsa, 128, KC * BW])
        wts[p + 'w_out'] = din(p + 'w_out', [nb_o, 128, KC * BW])
        wts[p + 'norm'] = din(p + 'norm', [128, KC])
        for kv in 'kv':
            wts[p + f'cmp_{kv}_w1'] = din(p + f'cmp_{kv}_w1', [128, 32 * 128])
            wts[p + f'cmp_{kv}_w2'] = din(p + f'cmp_{kv}_w2', [128, 128])
        wts[p + 'posT'] = din(p + 'posT', [128, 32])
    wts['w_kv'] = din('w_kv', [nb_sb, 128, KC * BW])
    wts['kv_norm'] = din('kv_norm', [128, KC])
    for l in (2, 3):
        p = f"b{l}_"
        wts[p + 'w_in'] = din(p + 'w_in', [nb_sb, 128, KC * BW])
        wts[p + 'w_out'] = din(p + 'w_out', [nb_o, 128, KC * BW])
        wts[p + 'norm'] = din(p + 'norm', [128, KC])
    fin_g = din('final_norm', [1, D])
    sband_d = din('sband', [H, 128, SB_X])
    wband_d = din('wband', [H, 128, WB_X])
    cband_d = din('cband', [H, NCMP, T])
    b31_d = din('b31', [128, H])
    mband_d = din('mband', [128, MB_X])
    topA_d = din('topA', [128, NT * NSLC])
    topC_d = din('topC', [128, NT * NSLC])
    E_d = din('Emat', [NSLC, NT * 128])
    ov_d = din('ovl', [NCMP, NSLC])
    out_d = nc.dram_tensor("out", [T, D], F32, kind="ExternalOutput").ap()

    xres = dscr("xres", [T, D], F32)
    featT = dscr("featT", [48, 128, T], BF16)
    vsH = dscr("vsH", [2, G, 128, NT * DH], BF16)
    gatesH = dscr("gatesH", [128, NT * 96], F32)
    szH = dscr("szH", [3, H, 128, NT * DH], BF16)
    kshT = dscr("kshT", [H, 128, T], BF16)
    vshH = dscr("vshH", [H, 128, NT * DH], BF16)
    mixT = dscr("mixT", [H, 128, T], BF16)
    c.dbg_out = {}

    st = ExitStack()
    with st:
        def sb(name, shape, dt):
            return st.enter_context(nc.sbuf_tensor(name, list(shape), dt))

        def psum(name, shape, dt):
            return st.enter_context(nc.psum_tensor(name, list(shape), dt))

        regA_t = sb("regA", [128, 65536], BF16)
        regW_t = sb("regW", [128, 16384], BF16)
        regS_t = sb("regS", [128, 8192], BF16)
        regC_t = sb("regC", [128, 7168], BF16)
        RA, RW, RS, RC = Region(regA_t[:]), Region(regW_t[:]), Region(regS_t[:]), Region(regC_t[:])
        PS = [psum(f"ps{i}", [128, 512], F32) for i in range(8)]
        PTB = PS[7][:, :].bitcast(BF16)

        identf = RC.take([128, 128], F32)
        identb = RC.take([128, 128], BF16)
        negU = RC.take([128, 128], BF16)
        negones = RC.take([128, 128], BF16)
        epsb = RC.take([128, 1], F32)
        gT = {k: RC.take([128, KC], F32) for k in ['a0_', 'a1_', 'kv_', 'b2_', 'b3_']}
        b31 = RC.take([128, H], F32)
        ss = RC.take([128, NT], F32)
        rs = RC.take([128, NT], F32)
        Emat = RC.take([128, NT, 128], BF16)
        ovl_f = RC.take([NCMP, NSLC], F32)
        onesf = RC.take([128, 128], F32)

        P = Prog(nc, st)
        P.op('pool', MSET(onesf, 1.0), writes=['onesf'])
        P.op('pool', lambda e: e.affine_select(out=identf, in_=onesf, pattern=[[-1, 128]], compare_op=ALU.is_equal,
                                               fill=0.0, base=0, channel_multiplier=1), reads=['onesf'], writes=['identf'])
        P.op('dve', CP(identb, identf), reads=['identf'], writes=['identb'])
        P.op('pool', MSET(negones, -1.0), writes=['negones'])
        P.op('pool', lambda e: e.affine_select(out=negU, in_=negones, pattern=[[-1, 128]], compare_op=ALU.is_ge,
                                               fill=0.0, base=0, channel_multiplier=1), reads=['negones'], writes=['negU'])
        P.op('dve', MSET(epsb, 1e-6), writes=['epsb'])
        for k, nm in [('a0_', 'a0_norm'), ('a1_', 'a1_norm'), ('kv_', 'kv_norm'), ('b2_', 'b2_norm'), ('b3_', 'b3_norm')]:
            P.dma('sp', DMA(gT[k], wts[nm]), writes=[('gT', k)])
        P.dma('sp', DMA(b31, b31_d), writes=['b31'])
        P.op('pool', MSET(Emat.rearrange("p a b -> p (a b)"), 0.0), writes=['Emat'])
        P.dma('pool', DMA(Emat[0:NSLC].rearrange("p a b -> p (a b)"), E_d), writes=['Emat'])
        P.dma('sp', DMA(ovl_f, ov_d), writes=['ovl_f'])

        c.P, c.PS, c.PTB = P, PS, PTB
        c.RA, c.RW, c.RS = RA, RW, RS
        c.identf, c.identb, c.negU, c.negones, c.epsb, c.gT, c.b31 = identf, identb, negU, negones, epsb, gT, b31
        c.ss, c.rs, c.Emat, c.ovl_f = ss, rs, Emat, ovl_f
        c.featT, c.vsH, c.gatesH, c.szH, c.kshT, c.vshH, c.mixT, c.xres = featT, vsH, gatesH, szH, kshT, vshH, mixT, xres
        c.sband_d, c.wband_d, c.cband_d, c.mband_d, c.topA_d, c.topC_d = sband_d, wband_d, cband_d, mband_d, topA_d, topC_d
        c.wts = wts

        def done(tag):
            return stop_after == tag

        finished = False
        xsrc = x_in
        if start_at == 'sb':
            sb_attention(c)
            finished = True
        for l in (range(2) if start_at != 'sb' else ()):
            p = f"a{l}_"
            if start_at is None:
                norm_phase(c, xsrc, gT[p])
            if done(f'n{l}'):
                finished = True
                break
            if start_at is None:
                proj_phase(c, wts[p + 'w_in'], nsa_blocks(), nsa_dest(c))
            if done(f'p{l}'):
                finished = True
                break
            compress_phase(c, p)
            nsa_attention(c, p)
            if done(f'at{l}'):
                finished = True
                break
            out_phase(c, wts[p + 'w_out'], xsrc, xres)
            xsrc = xres
            if done(f'o{l}'):
                finished = True
                break
        if not finished:
            norm_phase(c, xres, gT['kv_'])
            proj_phase(c, wts['w_kv'], sb_blocks('ksh', 'vsh'), sb_dest(c))
            for l in (2, 3):
                p = f"b{l}_"
                norm_phase(c, xres, gT[p])
                proj_phase(c, wts[p + 'w_in'], sb_blocks('q', 'z'), sb_dest(c))
                sb_attention(c)
                out_phase(c, wts[p + 'w_out'], xres, xres)
                if done(f'o{l}'):
                    finished = True
                    break
        if not finished:
            final_norm(c, xres, fin_g, out_d)
        P.barrier()
        P.finish()
        with nc.Block() as block:
            P.emit(block)
    return nc


def norm_phase(c, xsrc, gT):
    P, PS = c.P, c.PS
    P.barrier()
    c.RA.reset(); c.RW.reset(); c.RS.reset()
    hT = c.RA.take([128, KC, T], BF16)
    xt = [c.RW.take([128, D], F32) for _ in range(2)]
    junk = c.RS.take([128, D], F32)
    c.hT = hT
    P.op('dve', MSET(c.ss, 0.0), writes=['ss'])
    for tt in range(NT):
        x_ = xt[tt % 2]
        xk = ('xt', tt % 2)
        P.dma('sp', DMA(x_, xsrc[tt * 128:(tt + 1) * 128, :]), writes=[xk])
        sst, rst = c.ss[:, tt:tt + 1], c.rs[:, tt:tt + 1]
        P.op('act', ACT(junk, x_, AF.Square, accum=sst), reads=[xk, 'ss'], writes=['junk', ('ss', tt)])
        P.op('act', ACT(rst, sst, AF.Sqrt, bias=c.epsb[:, 0:1], scale=1.0 / D), reads=[('ss', tt), 'epsb'], writes=[('rs', tt)])
        P.op('dve', lambda e, o=rst: e.reciprocal(o, o), reads=[('rs', tt)], writes=[('rs', tt)])
        P.op('act', lambda e, o=x_, s=rst: e.mul(o, o, s), reads=[xk, ('rs', tt)], writes=[xk])
        for c4 in range(KC // 4):
            bank = 4 + c4 % 2
            ps = PS[bank]
            for k in range(4):
                kc = c4 * 4 + k
                P.op('pe', TR(ps[:, k * 128:(k + 1) * 128], x_[:, kc * 128:(kc + 1) * 128], c.identf),
                     reads=[xk, 'identf'], writes=[('ps', bank)], inc=(k == 3))
            for k in range(4):
                kc = c4 * 4 + k
                P.op('dve', TS(hT[:, kc, tt * 128:(tt + 1) * 128], ps[:, k * 128:(k + 1) * 128], gT[:, kc:kc + 1], None, ALU.mult),
                     reads=[('ps', bank), 'gTall'], writes=['hT'])


def nsa_dest(c):
    fid = {'q': 0, 'kc': 32, 'vc': 36, 'ks': 40, 'kw': 44}
    zid = {'zc': 0, 'zs': 1, 'zw': 2}

    def dest(kind, idx):
        if kind in fid:
            return c.featT[fid[kind] + idx], (QSCALE if kind == 'q' else None)
        if kind == 'vs':
            return c.vsH[0, idx], AF.Copy
        if kind == 'vw':
            return c.vsH[1, idx], AF.Copy
        if kind in zid:
            return c.szH[zid[kind], idx], AF.Silu
        raise KeyError(kind)
    return dest


def sb_dest(c):
    def dest(kind, idx):
        if kind == 'q':
            return c.featT[idx], QSCALE
        if kind == 'ksh':
            return c.kshT[idx], None
        if kind == 'vsh':
            return c.vshH[idx], AF.Copy
        if kind == 'z':
            return c.szH[0, idx], AF.Silu
        raise KeyError(kind)
    return dest


def proj_phase(c, wT, blocks, dest):
    P, PS = c.P, c.PS
    P.barrier()
    c.RW.reset(); c.RS.reset()
    hT = c.hT
    wbuf = [c.RW.take([128, KC, BW], BF16) for _ in range(2)]
    stage = [c.RS.take([128, 2, T], BF16) for _ in range(2)]
    pc = 0
    ev = 0
    for bi, (c0, wd, mode, kind, idx) in enumerate(blocks):
        wb = wbuf[bi % 2]
        wk = ('wbuf', bi % 2)
        P.dma('pool', DMA(wb.rearrange("p a b -> p (a b)"), wT[bi]), writes=[wk])
        sg = stage[bi % 2]
        sk = ('stage', bi % 2)
        if mode == 'B':
            for sub in range(wd // 128):
                dst, scale = dest(kind, idx + sub)
                for tg in range(4):
                    bank = pc % 4
                    pc += 1
                    ps = PS[bank]
                    for kc in range(KC):
                        P.op('pe', MM(ps[:, :], wb[:, kc, sub * 128:(sub + 1) * 128], hT[:, kc, tg * 512:(tg + 1) * 512], kc == 0, kc == KC - 1),
                             reads=['hT', wk], writes=[('ps', bank)], inc=(kc == KC - 1))
                    o = sg[:, sub, tg * 512:(tg + 1) * 512]
                    if ev % 2 == 0:
                        P.op('act', ACT(o, ps[:, :], AF.Copy, scale=(scale if scale is not None else 1.0)), reads=[('ps', bank)], writes=[sk])
                    else:
                        P.op('dve', TS(o, ps[:, :], (scale if scale is not None else 1.0), None, ALU.mult), reads=[('ps', bank)], writes=[sk])
                    ev += 1
                P.dma('sp', DMA(dst, sg[:, sub, :]), reads=[sk], writes=[('dst', kind, idx + sub)])
        elif kind == 'gates':
            gs = sg.rearrange("p a b -> p (a b)")[:, 0:NT * 96 * 2].bitcast(F32).rearrange("p (t k) -> p t k", k=96)
            for tt in range(NT):
                bank = pc % 4
                pc += 1
                ps = PS[bank]
                for kc in range(KC):
                    P.op('pe', MM(ps[:, 0:96], hT[:, kc, tt * 128:(tt + 1) * 128], wb[:, kc, 0:96], kc == 0, kc == KC - 1),
                         reads=['hT', wk], writes=[('ps', bank)], inc=(kc == KC - 1))
                P.op('act', ACT(gs[:, tt, :], ps[:, 0:96], AF.Sigmoid), reads=[('ps', bank)], writes=[sk])
            P.dma('sp', DMA(c.gatesH, gs.rearrange("p t k -> p (t k)")), reads=[sk], writes=['gatesH'])
        else:
            s4 = sg.rearrange("p h (t d) -> p h t d", d=128)
            func = dest(kind, idx)[1]
            for tt in range(NT):
                bank = pc % 4
                pc += 1
                ps = PS[bank]
                for kc in range(KC):
                    P.op('pe', MM(ps[:, 0:BW], hT[:, kc, tt * 128:(tt + 1) * 128], wb[:, kc, :], kc == 0, kc == KC - 1),
                         reads=['hT', wk], writes=[('ps', bank)], inc=(kc == KC - 1))
                o = s4[:, :, tt, :]
                i = ps[:, 0:BW].rearrange("p (h d) -> p h d", d=128)
                if func == AF.Copy and ev % 2 == 1:
                    P.op('dve', CP(o, i), reads=[('ps', bank)], writes=[sk])
                else:
                    P.op('act', ACT(o, i, func), reads=[('ps', bank)], writes=[sk])
                ev += 1
            for hh in range(2):
                P.dma('sp', DMA(dest(kind, idx + hh)[0], sg[:, hh, :]), reads=[sk], writes=[('dst', kind, idx + hh)])


def compress_phase(c, p):
    pass


def nsa_attention(c, p):
    P, PS, PTB = c.P, c.PS, c.PTB
    RA, RW, RS = c.RA, c.RW, c.RS
    P.barrier()
    RA.reset(); RW.reset(); RS.reset()
    identb, identf, b31 = c.identb, c.identf, c.b31
    kcT = RA.take([128, G, NCMP], BF16)
    vcov = RA.take([NCMP, G, 161], BF16)
    qT = [RA.take([128, T], BF16) for _ in range(2)]
    mixacc = [RA.take([128, NT, DH], F32) for _ in range(2)]
    ksT = RA.take([128, T], BF16)
    kwT = RA.take([128, T], BF16)
    vaug = [RA.take([128, NT, 129], BF16) for _ in range(2)]
    maskT = RA.take([128, T], BF16)
    impacc = RA.take([128, NT, NSLC], F32)
    topA = RA.take([128, NT, NSLC], F32)
    topC = RA.take([128, NT, NSLC], F32)
    gates = RA.take([128, NT, 96], F32)
    sz = [[RA.take([128, NT, DH], BF16) for _ in range(3)] for _ in range(2)]
    sband = [RA.take([128, SB_X], BF16) for _ in range(2)]
    wband = [RA.take([128, WB_X], BF16) for _ in range(2)]
    cband = [RA.take([NCMP, T], BF16) for _ in range(2)]
    pT = [RA.take([128, 512], BF16) for _ in range(6)]
    mixbf = [RA.take([128, DH], BF16) for _ in range(4)]
    mstage = [RA.take([128, T], BF16) for _ in range(2)]
    tmpf = [RA.take([128, DH], F32) for _ in range(4)]
    rzt = RA.take([128, 64], F32)
    wk = RA.take([128, NSLC], F32)
    wk2 = RA.take([128, NSLC], F32)
    mx8 = RA.take([128, 8], F32)
    w1b = RW.take([128, 32, 128], BF16)
    w2b = RW.take([128, 128], BF16)
    posTb = RW.take([128, 32], BF16)
    raw = [RW.take([128, T], BF16) for _ in range(2)]
    s1 = [RW.take([128, NCMP], BF16) for _ in range(2)]
    cpos = RW.take([128, 1], F32)

    P.op('pool', MSET(maskT, 0.0), writes=['maskT'])
    P.dma('sp', DMA(topA.rearrange("p a b -> p (a b)"), c.topA_d), writes=['topA'])
    P.dma('sp', DMA(topC.rearrange("p a b -> p (a b)"), c.topC_d), writes=['topC'])
    P.dma('sp', DMA(gates.rearrange("p a b -> p (a b)"), c.gatesH), writes=['gates'])
    P.dma('pool', DMA(posTb, c.wts[p + 'posT']), writes=['posTb'])
    for i in range(2):
        P.op('pool', MSET(vaug[i][:, :, 128:129], 1.0), writes=[('vaug1', i)])

    rc = 0
    for kv in 'kv':
        P.dma('pool', DMA(w1b.rearrange("p a b -> p (a b)"), c.wts[p + f'cmp_{kv}_w1']), writes=['w1b'])
        P.dma('pool', DMA(w2b, c.wts[p + f'cmp_{kv}_w2']), writes=['w2b'])
        for j in range(32):
            P.op('pe', MM(PS[6][:, 0:1], w1b[:, j, :], posTb[:, j:j + 1], j == 0, j == 31),
                 reads=['w1b', 'posTb'], writes=[('ps', 6)], inc=(j == 31))
        P.op('dve', CP(cpos, PS[6][:, 0:1]), reads=[('ps', 6)], writes=['cpos'])
        for g in range(G):
            rw = raw[rc % 2]
            rk = ('raw', rc % 2)
            s1_ = s1[rc % 2]
            sk = ('s1', rc % 2)
            rc += 1
            P.dma('sp', DMA(rw, c.featT[(32 if kv == 'k' else 36) + g]), writes=[rk])
            rv = rw.rearrange("p (n s) -> p n s", s=16)
            for j in range(32):
                P.op('pe', MM(PS[5][:, 0:NCMP], w1b[:, j, :], rv[:, (j // 16):(j // 16) + NCMP, j % 16], j == 0, j == 31),
                     reads=['w1b', rk], writes=[('ps', 5)], inc=(j == 31))
            P.op('act', ACT(s1_, PS[5][:, 0:NCMP], AF.Silu, bias=cpos[:, 0:1]), reads=[('ps', 5), 'cpos'], writes=[sk])
            if kv == 'k':
                P.op('pe', MM(PS[6][:, 0:NCMP], w2b, s1_, True, True), reads=['w2b', sk], writes=[('ps', 6)])
                P.op('dve', CP(kcT[:, g, :], PS[6][:, 0:NCMP]), reads=[('ps', 6)], writes=[('kcT', g)])
            else:
                P.op('pe', MM(PS[6][0:NCMP, 0:128], s1_, w2b, True, True), reads=['w2b', sk], writes=[('ps', 6)])
                P.op('dve', CP(vcov[:, g, 0:128], PS[6][0:NCMP, 0:128]), reads=[('ps', 6)], writes=[('vcov', g)])
    for g in range(G):
        P.op('pool', MSET(vcov[:, g, 128:129], 1.0), writes=[('vcov1', g)])
        P.op('dve', CP(vcov[:, g, 129:161], c.ovl_f), writes=[('vcovo', g)])

    if c.sub == 'cmp':
        c.dump('kcT', kcT.rearrange("p a b -> p (a b)"), [('kcT', g) for g in range(G)])
        c.dump('vcov', vcov.rearrange("p a b -> p (a b)"), [(n, g) for g in range(G) for n in ('vcov', 'vcov1', 'vcovo')])
        return
    st = {'s': 0, 'p': 0, 'c': 0, 'e': 0, 'i': 0, 'f': 0}

    class Pipe:
        def __init__(self, depth=1):
            self.q = []
            self.depth = depth
            self.deferred = []

        def _tick(self):
            keep = []
            for item in self.deferred:
                item[0] -= 1
                if item[0] <= 0:
                    item[1]()
                else:
                    keep.append(item)
            self.deferred = keep

        def push(self, a, b):
            a()
            self.q.append(b)
            if len(self.q) > self.depth:
                self.q.pop(0)()
            self._tick()

        def defer(self, fn, n):
            self.deferred.append([n, fn])

        def flush(self):
            while self.q:
                self.q.pop(0)()
            for item in self.deferred:
                item[1]()
            self.deferred = []

    def cmp_tile(g, qsl, cb, Q, qk, ck, sbanks=(0, 1)):
        bank = sbanks[st['s'] % len(sbanks)]
        st['s'] += 1
        psS = PS[bank]
        P.op('pe', MM(psS[0:NCMP, :], kcT[:, g, :], qsl, True, False), reads=[('kcT', g), qk], writes=[('ps', bank)], inc=False)
        P.op('pe', MM(psS[0:NCMP, :], identb[0:NCMP, 0:NCMP], cb[:, Q * 512:(Q + 1) * 512], False, True),
             reads=[ck], writes=[('ps', bank)])
        pi = st['p'] % len(pT)
        st['p'] += 1
        pt = pT[pi][0:NCMP, :]
        P.op('act', ACT(pt, psS[0:NCMP, :], AF.Exp), reads=[('ps', bank)], writes=[('pT', pi)])
        return pt, ('pT', pi)

    def pv_banks():
        rr = st['c']
        st['c'] += 1
        return (2, 3) if rr % 2 == 0 else (4, 5)

    def load1(h):
        P.dma('sp', DMA(qT[h % 2], c.featT[h]), writes=[('qT', h % 2)])
        P.dma('pool', DMA(cband[h % 2], c.cband_d[h]), writes=[('cband', h % 2)])

    def load3(h):
        load1(h)
        P.dma('pool', DMA(sband[h % 2], c.sband_d[h]), writes=[('sband', h % 2)])
        P.dma('pool', DMA(wband[h % 2], c.wband_d[h]), writes=[('wband', h % 2)])
        for br in range(3):
            P.dma('sp', DMA(sz[h % 2][br].rearrange("p a b -> p (a b)"), c.szH[br, h]), writes=[('sz', h % 2, br)])

    def evac_round(h, br, banks, Q):
        macc = mixacc[h % 2]
        szh = sz[h % 2]
        for half in range(2):
            b_ = banks[half]
            pok = ('ps', b_)
            tt0 = 4 * Q + 2 * half
            k = 2 * (st['e'] % 8)
            st['e'] += 1
            rz = rzt[:, 32 + k:34 + k]
            rzk = ('rz', k)
            zc = PS[b_][:, :].rearrange("p (s c) -> p s c", c=256)[:, :, 128]
            P.op('dve', TS(rz, zc, 1e-30, None, ALU.max), reads=[pok], writes=[rzk])
            P.op('dve', lambda e, o=rz: e.reciprocal(o, o), reads=[rzk], writes=[rzk])
            P.op('dve', TT(rz, rz, gates[:, tt0:tt0 + 2, br * H + h], ALU.mult), reads=[rzk, 'gates'], writes=[rzk])
            for j in range(2):
                tt = tt0 + j
                po = PS[b_][:, j * 256:j * 256 + 128]
                if br == 0:
                    P.op('dve', STT(macc[:, tt, :], po, rz[:, j:j + 1], szh[br][:, tt, :], ALU.mult, ALU.mult),
                         reads=[pok, rzk, ('sz', h % 2, br)], writes=[('macc', h % 2, tt)])
                else:
                    ti = st['f'] % 4
                    st['f'] += 1
                    tf, tk = tmpf[ti], ('tmpf', ti)
                    P.op('dve', STT(tf, po, rz[:, j:j + 1], szh[br][:, tt, :], ALU.mult, ALU.mult),
                         reads=[pok, rzk, ('sz', h % 2, br)], writes=[tk])
                    if br == 1:
                        P.op('pool', TT(macc[:, tt, :], macc[:, tt, :], tf, ALU.add), reads=[tk, ('macc', h % 2, tt)], writes=[('macc', h % 2, tt)])
                    else:
                        mb, mbk = mixbf[tt % 4], ('mixbf', tt % 4)
                        P.op('pool', TT(mb, macc[:, tt, :], tf, ALU.add), reads=[tk, ('macc', h % 2, tt)], writes=[mbk])

    def finalize(h, Q):
        mst, mstk = mstage[h % 2], ('mstage', h % 2)
        for qs in range(4):
            tt = Q * 4 + qs
            mb, mbk = mixbf[tt % 4], ('mixbf', tt % 4)
            P.op('pe', TR(PTB[:, qs * 128:(qs + 1) * 128], mb, identb), reads=[mbk], writes=[('ps', 7)])
        P.op('dve', CP(mst[:, Q * 512:(Q + 1) * 512], PTB[:, 0:512]), reads=[('ps', 7)], writes=[mstk])
        if Q == NQ - 1:
            P.dma('sp', DMA(c.mixT[h], mst), reads=[mstk], writes=[('mixT', h)])

    for g in range(G):
        P.dma('sp', DMA(ksT, c.featT[40 + g]), writes=['ksT'])
        P.dma('sp', DMA(kwT, c.featT[44 + g]), writes=['kwT'])
        for i in range(2):
            P.dma('sp', DMA(vaug[i][:, :, 0:128], c.vsH[i, g].rearrange("p (t d) -> p t d", d=128)), writes=[('vaug', i)])
        pipe = Pipe()
        load1(g * R)
        for r in range(R):
            h = g * R + r
            if r + 1 < R:
                load1(h + 1)
            for Q in range(NQ):
                hold = {}

                def A(h=h, Q=Q, hold=hold):
                    hold['pt'], hold['pk'] = cmp_tile(g, qT[h % 2][:, Q * 512:(Q + 1) * 512], cband[h % 2], Q, ('qT', h % 2), ('cband', h % 2))

                def B(h=h, Q=Q, r=r, hold=hold):
                    pt, pk = hold['pt'], hold['pk']
                    bank = 6 + st['i'] % 2
                    st['i'] += 1
                    bkey = ('ps', bank)
                    for qs in range(4):
                        po = PS[bank][:, qs * 64:qs * 64 + 33]
                        P.op('pe', MM(po, pt[:, qs * 128:(qs + 1) * 128], vcov[:, g, 128:161], qs == 0, True, skip=True),
                             reads=[pk, ('vcov1', g), ('vcovo', g)], writes=[bkey], inc=(qs == 3))
                    for qs in range(4):
                        tt = Q * 4 + qs
                        po = PS[bank][:, qs * 64:qs * 64 + 33]
                        rz = rzt[:, 16 + 4 * (st['i'] % 2) + qs:17 + 4 * (st['i'] % 2) + qs]
                        rzk = ('rz1', st['i'] % 2, qs)
                        P.op('dve', TS(rz, po[:, 0:1], 1e-30, None, ALU.max), reads=[bkey], writes=[rzk])
                        P.op('dve', lambda e, o=rz: e.reciprocal(o, o), reads=[rzk], writes=[rzk])
                        if r == 0:
                            P.op('dve', TS(impacc[:, tt, :], po[:, 1:33], rz, None, ALU.mult), reads=[bkey, rzk], writes=[('imp', tt)])
                        else:
                            P.op('dve', STT(impacc[:, tt, :], po[:, 1:33], rz, impacc[:, tt, :], ALU.mult, ALU.add),
                                 reads=[bkey, rzk, ('imp', tt)], writes=[('imp', tt)])
                pipe.push(A, B)
        pipe.flush()
        if c.sub == 's1':
            c.dump('impacc', impacc.rearrange("p a b -> p (a b)"), [('imp', tt) for tt in range(NT)])
            return
        load3(g * R)
        for tt in range(NT):
            P.op('dve', TT(wk, impacc[:, tt, :], topA[:, tt, :], ALU.mult), reads=[('imp', tt), 'topA'], writes=['wk'])
            P.op('dve', TT(wk, wk, topC[:, tt, :], ALU.add), reads=['wk', 'topC'], writes=['wk'])
            P.op('dve', lambda e: e.max(out=mx8, in_=wk), reads=['wk'], writes=['mx8'])
            P.op('dve', lambda e: e.match_replace(out=wk2, in_to_replace=mx8, in_values=wk, imm_value=-1e9), reads=['wk', 'mx8'], writes=['wk2'])
            P.op('dve', lambda e: e.max(out=mx8, in_=wk2), reads=['wk2'], writes=['mx8'])
            P.op('dve', lambda e: e.match_replace(out=wk2, in_to_replace=mx8, in_values=wk2, imm_value=-1e9), reads=['wk2', 'mx8'], writes=['wk2'])
            P.op('dve', TS(wk, wk2, -1e8, None, ALU.is_le), reads=['wk2'], writes=['wk'])
            P.op('dve', TS(wk, wk, -1.0, -NEG, ALU.add, ALU.mult), reads=['wk'], writes=['wk'])
            P.op('pe', TR(PS[6][0:NSLC, 0:128], wk, identf), reads=['wk'], writes=[('ps', 6)])
            P.op('dve', CP(maskT[0:NSLC, tt * 128:(tt + 1) * 128], PS[6][0:NSLC, 0:128]), reads=[('ps', 6)], writes=['maskT'])
        if c.sub == 's2':
            c.dump('maskT', maskT[0:NSLC, :], ['maskT'])
            return
        pipe = Pipe(2)
        SB3 = (0, 1, 6)
        for r in range(R):
            h = g * R + r
            tile_no = 0
            for Q in range(NQ):
                rounds = [(0, [0])]
                rounds.append((1, list(range(0, 4 * Q + 4))))
                rounds.append((2, list(range(max(0, 4 * Q - 4), 4 * Q + 4))))
                for br, kts in rounds:
                    rctx = {}
                    for ki, kt in enumerate(kts):
                        hold = {}
                        is_first = (ki == 0)
                        is_last = (ki == len(kts) - 1)
                        pre = (tile_no == 2) and r + 1 < R
                        tile_no += 1

                        def A(h=h, Q=Q, br=br, kt=kt, hold=hold, pre=pre):
                            if pre:
                                load3(h + 1)
                            q_, qk = qT[h % 2], ('qT', h % 2)
                            qsl = q_[:, Q * 512:(Q + 1) * 512]
                            if br == 0:
                                hold['pt'], hold['pk'] = cmp_tile(g, qsl, cband[h % 2], Q, qk, ('cband', h % 2), SB3)
                                return
                            kT_, kk = (ksT, 'ksT') if br == 1 else (kwT, 'kwT')
                            bd, bk = (sband[h % 2], ('sband', h % 2)) if br == 1 else (wband[h % 2], ('wband', h % 2))
                            Dd = Q * 512 - kt * 128
                            far = (br == 1 and Dd >= 256)
                            bank = SB3[st['s'] % 3]
                            st['s'] += 1
                            psS = PS[bank]
                            P.op('pe', MM(psS[:, :], kT_[:, kt * 128:(kt + 1) * 128], qsl, True, False), reads=[kk, qk], writes=[('ps', bank)], inc=False)
                            if br == 1 and os.environ.get('K_EXP_NOE'):
                                if far:
                                    P.op('pe', MM(psS[:, :], identb, bd[:, 0:512], False, True), reads=[bk], writes=[('ps', bank)])
                            elif br == 1:
                                P.op('pe', MM(psS[:, :], c.Emat[:, kt, :], maskT[:, Q * 512:(Q + 1) * 512], False, far),
                                     reads=['maskT'], writes=[('ps', bank)], inc=far)
                            if not far:
                                x0 = Dd + 384
                                P.op('pe', MM(psS[:, :], identb, bd[:, x0:x0 + 512], False, True), reads=[bk], writes=[('ps', bank)])
                            pi = st['p'] % len(pT)
                            st['p'] += 1
                            P.op('act', ACT(pT[pi], psS[:, :], AF.Exp, bias=(b31[:, h:h + 1] if far else None)), reads=[('ps', bank)], writes=[('pT', pi)])
                            hold['pt'], hold['pk'] = pT[pi], ('pT', pi)

                        def B(h=h, Q=Q, br=br, kt=kt, hold=hold, rctx=rctx, is_first=is_first, is_last=is_last):
                            if is_first:
                                rctx['banks'] = pv_banks()
                            banks = rctx['banks']

                            def slot(qs):
                                b_ = banks[qs // 2]
                                return PS[b_][:, (qs % 2) * 256:(qs % 2) * 256 + 129], ('ps', b_)
                            pt, pk = hold['pt'], hold['pk']
                            if br == 0:
                                for qs in range(4):
                                    po, pok = slot(qs)
                                    P.op('pe', MM(po, pt[:, qs * 128:(qs + 1) * 128], vcov[:, g, 0:129], qs % 2 == 0, True, skip=True),
                                         reads=[pk, ('vcov', g), ('vcov1', g)], writes=[pok], inc=(qs % 2 == 1))
                            else:
                                vg = br - 1
                                for qs in range(4):
                                    tq = 4 * Q + qs
                                    lo = 0 if br == 1 else max(0, tq - 4)
                                    if kt < lo or kt > tq:
                                        continue
                                    po, pok = slot(qs)
                                    if os.environ.get('K_EXP_N128'):
                                        po = po[:, 0:128]
                                    P.op('pe', MM(po, pt[:, qs * 128:(qs + 1) * 128], (vaug[vg][:, kt, 0:128] if os.environ.get('K_EXP_N128') else vaug[vg][:, kt, :]), (kt == lo and qs % 2 == 0), kt == tq, skip=True),
                                         reads=[pk, ('vaug', vg), ('vaug1', vg)], writes=[pok], inc=(kt == tq))
                            if is_last:
                                evac_round(h, br, banks, Q)
                                if br == 2:
                                    pipe.defer(lambda h=h, Q=Q: finalize(h, Q), 3)
                        pipe.push(A, B)
            if c.sub and c.sub.startswith('s3'):
                pipe.flush()
                return
        pipe.flush()


def out_phase(c, wT, xsrc, xdst):
    P, PS = c.P, c.PS
    P.barrier()
    c.RA.reset(); c.RW.reset(); c.RS.reset()
    mT = c.RA.take([128, KC, T], BF16)
    wbuf = [c.RW.take([128, KC, BW], BF16) for _ in range(2)]
    ost = c.RS.take([128, NT, BW], F32)
    for h in range(H):
        P.dma('sp', DMA(mT[:, h, :], c.mixT[h]), writes=[('mT', h)])
    xs3 = xsrc.rearrange("(t p) c -> p t c", p=128)
    xd3 = xdst.rearrange("(t p) c -> p t c", p=128)
    pc = 0
    for bi in range(D // BW):
        wb, wk = wbuf[bi % 2], ('wbuf', bi % 2)
        P.dma('pool', DMA(wb.rearrange("p a b -> p (a b)"), wT[bi]), writes=[wk])
        P.dma('sp', DMA(ost, xs3[:, :, bi * BW:(bi + 1) * BW]), reads=[('xcol', bi)], writes=['ost'])
        for tt in range(NT):
            bank = pc % 4
            pc += 1
            ps = PS[bank]
            for kc in range(KC):
                P.op('pe', MM(ps[:, 0:BW], mT[:, kc, tt * 128:(tt + 1) * 128], wb[:, kc, :], kc == 0, kc == KC - 1),
                     reads=[('mT', kc), wk], writes=[('ps', bank)], inc=(kc == KC - 1))
            P.op('dve', TT(ost[:, tt, :], ps[:, 0:BW], ost[:, tt, :], ALU.add), reads=[('ps', bank), 'ost'], writes=['ost'])
        P.dma('sp', DMA(xd3[:, :, bi * BW:(bi + 1) * BW], ost), reads=['ost'], writes=[('xcol', bi)])


def sb_attention(c):
    P, PS, PTB = c.P, c.PS, c.PTB
    RA, RW, RS = c.RA, c.RW, c.RS
    P.barrier()
    RA.reset(); RW.reset(); RS.reset()
    identb, negU, negones = c.identb, c.negU, c.negones
    qT = [RA.take([128, T], BF16) for _ in range(2)]
    kT = [RA.take([128, T], BF16) for _ in range(2)]
    vh = [RA.take([128, NT, DH], BF16) for _ in range(2)]
    szh = [RA.take([128, NT, DH], BF16) for _ in range(2)]
    mband = RA.take([128, MB_X], BF16)
    e_f = [RA.take([128, 512], F32) for _ in range(2)]
    sp_b = [RA.take([128, 512], BF16) for _ in range(3)]
    spm_b = [RA.take([128, 512], BF16) for _ in range(3)]
    sacc_f = RA.take([128, 512], F32)
    sacc_b = [RA.take([128, 512], BF16) for _ in range(3)]
    a_b = [RA.take([128, 512], BF16) for _ in range(4)]
    mixbf = [RA.take([128, DH], BF16) for _ in range(4)]
    mstage = [RA.take([128, T], BF16) for _ in range(2)]
    P.dma('pool', DMA(mband, c.mband_d), writes=['mband'])
    P.op('dve', TS(mband, mband, -1.0, -NEG, ALU.add, ALU.mult), reads=['mband'], writes=['mband'])
    st = {'t': 0, 'a': 0, 'w': 0, 'r': 0}

    def loads(h):
        P.dma('sp', DMA(qT[h % 2], c.featT[h]), writes=[('qT', h % 2)])
        P.dma('sp', DMA(kT[h % 2], c.kshT[h]), writes=[('kT', h % 2)])
        P.dma('sp', DMA(vh[h % 2].rearrange("p a b -> p (a b)"), c.vshH[h]), writes=[('vh', h % 2)])
        P.dma('sp', DMA(szh[h % 2].rearrange("p a b -> p (a b)"), c.szH[0, h]), writes=[('szh', h % 2)])

    q3 = [None, None]
    deferred = []

    def tick():
        keep = []
        for item in deferred:
            item[0] -= 1
            if item[0] <= 0:
                item[1]()
            else:
                keep.append(item)
        deferred[:] = keep

    def push(A, B, C):
        A()
        if q3[0] is not None:
            q3[0][0]()
        if q3[1] is not None:
            q3[1][1]()
        q3[1] = q3[0]
        q3[0] = (B, C)
        tick()

    def flush():
        if q3[0] is not None:
            q3[0][0]()
        if q3[1] is not None:
            q3[1][1]()
        if q3[0] is not None:
            q3[0][1]()
        q3[0] = q3[1] = None
        for item in deferred:
            item[1]()
        deferred[:] = []

    loads(0)
    for h in range(H):
        q_, qk = qT[h % 2], ('qT', h % 2)
        k_, kk = kT[h % 2], ('kT', h % 2)
        tile_no = 0
        for Q in range(NQ):
            kts = list(range(4 * Q + 3, -1, -1))
            n = len(kts)
            rc = {}
            for i, kt in enumerate(kts):
                hold = {}
                pre = (tile_no == 6) and h + 1 < H
                tile_no += 1

                def A(h=h, Q=Q, i=i, kt=kt, n=n, hold=hold, rc=rc, pre=pre, q_=q_, qk=qk, k_=k_, kk=kk):
                    if pre:
                        loads(h + 1)
                    qsl = q_[:, Q * 512:(Q + 1) * 512]
                    ksl = k_[:, kt * 128:(kt + 1) * 128]
                    Dd = Q * 512 - kt * 128
                    diag = Dd < 128
                    t = st['t']
                    st['t'] += 1
                    b0 = t % 2
                    psL = PS[b0]
                    if diag:
                        x0 = Dd + 384
                        P.op('pe', MM(psL[:, :], ksl, qsl, True, False), reads=[kk, qk], writes=[('ps', b0)], inc=False)
                        P.op('pe', MM(psL[:, :], identb, mband[:, x0:x0 + 512], False, True), reads=['mband'], writes=[('ps', b0)])
                    else:
                        P.op('pe', MM(psL[:, :], ksl, qsl, True, True), reads=[kk, qk], writes=[('ps', b0)])
                    ef, efk = e_f[b0], ('e_f', b0)
                    P.op('act', ACT(ef, psL[:, :], AF.Exp), reads=[('ps', b0)], writes=[efk])
                    w3 = t % 3
                    spb, spk = sp_b[w3], ('sp_b', w3)
                    P.op('act', ACT(spb, ef, AF.Ln, bias=1.0), reads=[efk], writes=[spk])
                    spm, smk = spb, spk
                    hold.update(spm=spm, smk=smk, diag=diag, Dd=Dd, qsl=qsl, ksl=ksl)
                    if i == 0:
                        rc['sacc'] = (spm, smk)
                        if n > 2:
                            P.op('dve', CP(sacc_f, spm), reads=[smk], writes=['sacc_f'])
                    else:
                        hold['sacc_prev'] = rc['sacc']
                        if i + 1 < n:
                            wb = st['w'] % 3
                            st['w'] += 1
                            P.op('dve', TT(sacc_b[wb], sacc_f, spm, ALU.add), reads=['sacc_f', smk], writes=[('sacc_b', wb)])
                            rc['sacc'] = (sacc_b[wb], ('sacc_b', wb))
                        if i + 2 < n:
                            P.op('dve', TT(sacc_f, sacc_f, spm, ALU.add), reads=['sacc_f', smk], writes=['sacc_f'])

                def B(i=i, hold=hold, qk=qk, kk=kk):
                    t = st['a']
                    st['a'] += 1
                    b0 = 2 + t % 2
                    psB = PS[b0]
                    spm, smk = hold['spm'], hold['smk']
                    P.op('pe', MM(psB[:, :], hold['ksl'], hold['qsl'], True, False), reads=[kk, qk], writes=[('ps', b0)], inc=False)
                    if hold['diag']:
                        x0 = hold['Dd'] + 384
                        P.op('pe', MM(psB[:, :], identb, mband[:, x0:x0 + 512], False, False), reads=['mband'], writes=[('ps', b0)], inc=False)
                    P.op('pe', MM(psB[:, :], negU, spm, False, i == 0), reads=[smk], writes=[('ps', b0)], inc=(i == 0))
                    if i > 0:
                        sa, sak = hold['sacc_prev']
                        P.op('pe', MM(psB[:, :], negones, sa, False, True), reads=[sak], writes=[('ps', b0)])
                    w4 = t % 4
                    ab, abk = a_b[w4], ('a_b', w4)
                    P.op('act', ACT(ab, psB[:, :], AF.Exp), reads=[('ps', b0)], writes=[abk])
                    hold.update(ab=ab, abk=abk)

                def C(h=h, Q=Q, i=i, kt=kt, n=n, hold=hold, rc=rc):
                    if i == 0:
                        rc['pb'] = 4 + st['r'] % 2
                        st['r'] += 1
                    pb = rc['pb']
                    pso = PS[pb]
                    v_, vk = vh[h % 2], ('vh', h % 2)
                    ab, abk = hold['ab'], hold['abk']
                    for qs in range(4):
                        tq = 4 * Q + qs
                        if kt > tq:
                            continue
                        po = pso[:, qs * 128:(qs + 1) * 128]
                        P.op('pe', MM(po, ab[:, qs * 128:(qs + 1) * 128], v_[:, kt, :], (i == 0 and qs == 3), kt == 0, skip=True),
                             reads=[abk, vk], writes=[('ps', pb)], inc=(kt == 0))
                    if i == n - 1:
                        def epilogue(h=h, Q=Q, pb=pb):
                            pso_ = PS[pb]
                            z_, zk = szh[h % 2], ('szh', h % 2)
                            mst, mstk = mstage[h % 2], ('mstage', h % 2)
                            for qs in range(4):
                                tt = 4 * Q + qs
                                mb, mbk = mixbf[qs], ('mixbf', qs)
                                P.op('dve', TT(mb, pso_[:, qs * 128:(qs + 1) * 128], z_[:, tt, :], ALU.mult), reads=[('ps', pb), zk], writes=[mbk])
                                P.op('pe', TR(PTB[:, qs * 128:(qs + 1) * 128], mb, identb), reads=[mbk], writes=[('ps', 7)])
                            P.op('dve', CP(mst[:, Q * 512:(Q + 1) * 512], PTB[:, 0:512]), reads=[('ps', 7)], writes=[mstk])
                            if Q == NQ - 1:
                                P.dma('sp', DMA(c.mixT[h], mst), reads=[mstk], writes=[('mixT', h)])
                        deferred.append([3, epilogue])
                push(A, B, C)
    flush()


def final_norm(c, xsrc, fin_g, out_d):
    P = c.P
    P.barrier()
    c.RA.reset()
    gB = c.RA.take([128, D], F32)
    xt = [c.RA.take([128, D], F32) for _ in range(2)]
    junk = c.RA.take([128, D], F32)
    P.dma('sp', DMA(gB, fin_g.partition_broadcast(128)), writes=['gB'])
    P.op('dve', MSET(c.ss, 0.0), writes=['ss'])
    for tt in range(NT):
        x_, xk = xt[tt % 2], ('xt', tt % 2)
        P.dma('sp', DMA(x_, xsrc[tt * 128:(tt + 1) * 128, :]), writes=[xk])
        sst, rst = c.ss[:, tt:tt + 1], c.rs[:, tt:tt + 1]
        P.op('act', ACT(junk, x_, AF.Square, accum=sst), reads=[xk, 'ss'], writes=['junk', ('ss', tt)])
        P.op('act', ACT(rst, sst, AF.Sqrt, bias=c.epsb[:, 0:1], scale=1.0 / D), reads=[('ss', tt)], writes=[('rs', tt)])
        P.op('dve', lambda e, o=rst: e.reciprocal(o, o), reads=[('rs', tt)], writes=[('rs', tt)])
        P.op('dve', STT(x_, x_, rst, gB, ALU.mult, ALU.mult), reads=[xk, ('rs', tt), 'gB'], writes=[xk])
        P.dma('sp', DMA(out_d[tt * 128:(tt + 1) * 128, :], x_), reads=[xk], writes=[('out', tt)])


def prep_shared(inp):
    sh = {}
    nb, sbq, ob = nsa_blocks(), sb_blocks('q', 'z'), out_blocks()

    def gT(v):
        return np.ascontiguousarray(np.asarray(v, np.float32).reshape(KC, 128).T)
    for l in range(2):
        p = f"a{l}_"
        sh[p + 'w_in'] = tile_weight(np.asarray(inp[p + 'w_in'], np.float32), nb)
        sh[p + 'w_out'] = tile_weight(np.asarray(inp[p + 'w_out'], np.float32), ob)
        sh[p + 'norm'] = gT(inp[p + 'norm'])
        for kv in 'kv':
            w1 = np.asarray(inp[p + f'cmp_{kv}_w1'], np.float32)
            sh[p + f'cmp_{kv}_w1'] = np.ascontiguousarray(w1.reshape(32, 128, 128).transpose(1, 0, 2).reshape(128, 32 * 128))
            sh[p + f'cmp_{kv}_w2'] = np.ascontiguousarray(np.asarray(inp[p + f'cmp_{kv}_w2'], np.float32))
        sh[p + 'posT'] = np.ascontiguousarray(np.asarray(inp[p + 'cmp_pos'], np.float32).T)
    sh['w_kv'] = tile_weight(np.asarray(inp['w_kv'], np.float32), sbq)
    sh['kv_norm'] = gT(inp['kv_norm'])
    for l in (2, 3):
        p = f"b{l}_"
        sh[p + 'w_in'] = tile_weight(np.asarray(inp[p + 'w_in'], np.float32), sbq)
        sh[p + 'w_out'] = tile_weight(np.asarray(inp[p + 'w_out'], np.float32), ob)
        sh[p + 'norm'] = gT(inp[p + 'norm'])
    sh['final_norm'] = np.ascontiguousarray(np.asarray(inp['final_norm'], np.float32).reshape(1, D))
    sband, wband, cband, b31 = host_tables(inp['rel_bias'])
    sh['sband'], sh['wband'], sh['cband'], sh['b31'] = sband, wband, cband, b31
    mband, topA, topC, E, ov = const_tables()
    sh['mband'] = mband
    sh['topA'] = topA.reshape(128, NT * NSLC)
    sh['topC'] = topC.reshape(128, NT * NSLC)
    sh['Emat'] = E
    sh['ovl'] = ov
    return sh


def kernel(**inputs):
    n = 8
    x = np.asarray(inputs['x'], np.float32)
    sh = prep_shared(inputs)
    nc = build()
    in_maps = []
    for b in range(n):
        m = dict(sh)
        m['x'] = np.ascontiguousarray(x[b])
        in_maps.append(m)
    res = run_bass_kernel_spmd(nc, in_maps, core_ids=list(range(n)))
    return np.stack([r['out'] for r in res.results], axis=0).astype(np.float32)
```

```python
import math
import os
from contextlib import ExitStack

import numpy as np
import concourse.bass as bass
import concourse.mybir as mybir
from concourse.bass_utils import run_bass_kernel_spmd

F32 = mybir.dt.float32
BF16 = mybir.dt.bfloat16
ALU = mybir.AluOpType
AF = mybir.ActivationFunctionType

T, D, H, G, R, DH = 2048, 4096, 32, 4, 8, 128
NT, NQ, KC = 16, 4, 32
NCMP, NSLC, NSEL = 127, 32, 16
NSA_IN = 19552
NEG = -30000.0
QSCALE = 1.0 / math.sqrt(DH)
BW = 256
SB_X = 1152
WB_X = 1408
MB_X = 896

ENGS = ['pe', 'act', 'dve', 'pool', 'sp']
NDS = 8


class Prog:
    def __init__(self, nc, stack):
        self.nc = nc
        self.ops = {e: [] for e in ENGS}
        self.cnt = {e: 0 for e in ENGS}
        self.pending = {e: False for e in ENGS}
        self.semh = {}
        for e in ['pe', 'act', 'dve', 'pool']:
            self.semh[('c', e)] = stack.enter_context(nc.semaphore(f"c_{e}"))
        self.dcnt = {}
        self.drr = {}
        for q in ['sp', 'act', 'pool']:
            self.drr[q] = 0
            for i in range(NDS):
                self.semh[('d', q, i)] = stack.enter_context(nc.semaphore(f"d_{q}{i}"))
                self.dcnt[(q, i)] = 0
        self.seen = {e: {} for e in ENGS}
        self.res = {}

    def _deps(self, reads, writes):
        deps = {}

        def add(k, v):
            if deps.get(k, 0) < v:
                deps[k] = v
        for r in reads:
            st = self.res.get(r)
            if st and st[0]:
                add(*st[0])
        for w in writes:
            st = self.res.get(w)
            if st:
                if st[0]:
                    add(*st[0])
                for k, v in st[1].items():
                    add(k, v)
        return deps

    def _commit(self, ev, reads, writes):
        k, v = ev
        for r in reads:
            st = self.res.setdefault(r, [None, {}])
            if st[1].get(k, 0) < v:
                st[1][k] = v
        for w in writes:
            self.res[w] = [ev, {}]

    def _waits(self, eng, deps):
        waits = []
        for k, v in deps.items():
            if eng == 'pe' and k == ('c', 'pe'):
                continue
            if self.seen[eng].get(k, 0) >= v:
                continue
            self.seen[eng][k] = v
            waits.append((k, v))
        return waits

    def op(self, eng, fn, reads=(), writes=(), inc=True):
        deps = self._deps(reads, writes)
        waits = self._waits(eng, deps)
        if inc:
            self.cnt[eng] += 1
            ev = (('c', eng), self.cnt[eng])
            self.pending[eng] = False
        else:
            ev = (('c', eng), self.cnt[eng] + 1)
            self.pending[eng] = True
        self.ops[eng].append((fn, waits, ev if inc else None))
        self._commit(ev, reads, writes)

    def dma(self, q, fn, reads=(), writes=()):
        deps = self._deps(reads, writes)
        i = self.drr[q]
        self.drr[q] = (i + 1) % NDS
        k = ('d', q, i)
        prev = self.dcnt[(q, i)]
        if prev > 0 and deps.get(k, 0) < prev:
            deps[k] = prev
        waits = self._waits(q, deps)
        self.dcnt[(q, i)] += 16
        ev = (k, self.dcnt[(q, i)])
        self.ops[q].append((fn, waits, ev))
        self._commit(ev, reads, writes)

    def _all_targets(self):
        t = []
        for (q, i), v in self.dcnt.items():
            if v > 0:
                t.append((('d', q, i), v))
        for e in ['pe', 'act', 'dve', 'pool']:
            if self.cnt[e] > 0:
                t.append((('c', e), self.cnt[e]))
        return t

    def barrier(self):
        for e in ENGS:
            assert not self.pending[e], e
        tg = self._all_targets()
        for e in ENGS:
            waits = []
            for k, v in tg:
                if e == 'pe' and k == ('c', 'pe'):
                    continue
                if self.seen[e].get(k, 0) >= v:
                    continue
                self.seen[e][k] = v
                waits.append((k, v))
            if waits:
                self.ops[e].append((None, waits, None))
        self.res.clear()

    def finish(self):
        for e in ENGS:
            assert not self.pending[e], e
        self.ops['sp'].append((None, self._all_targets(), None))

    def emit(self, block):
        def run(name, e):
            for fn, waits, ev in self.ops[name]:
                for k, v in waits:
                    e.wait_ge(self.semh[k], v)
                if fn is None:
                    continue
                ins = fn(e)
                if ev is not None:
                    k, v = ev
                    ins.then_inc(self.semh[k], 16 if k[0] == 'd' else 1)

        @block.tensor
        def _(e):
            run('pe', e)

        @block.scalar
        def _(e):
            run('act', e)

        @block.vector
        def _(e):
            run('dve', e)

        @block.gpsimd
        def _(e):
            run('pool', e)

        @block.sync
        def _(e):
            run('sp', e)


def MM(o, l, r, st, sp, skip=False):
    if skip:
        return lambda e: e.matmul(o, l, r, start=st, stop=sp, skip_group_check=True)
    return lambda e: e.matmul(o, l, r, start=st, stop=sp)


def TR(o, i, idn):
    return lambda e: e.transpose(o, i, idn)


def ACT(o, i, f, bias=None, scale=None, accum=None):
    kw = {}
    if bias is not None:
        kw['bias'] = bias
    if scale is not None:
        kw['scale'] = scale
    if accum is not None:
        kw['accum_out'] = accum
    return lambda e: e.activation(o, i, f, **kw)


def TS(o, i, s1, s2, op0, op1=None):
    if op1 is None:
        return lambda e: e.tensor_scalar(o, i, s1, None, op0)
    return lambda e: e.tensor_scalar(o, i, s1, s2, op0, op1)


def TT(o, a, b, op):
    return lambda e: e.tensor_tensor(o, a, b, op)


def STT(o, i0, s, i1, op0, op1):
    return lambda e: e.scalar_tensor_tensor(o, i0, s, i1, op0, op1)


def CP(o, i):
    return lambda e: e.tensor_copy(o, i)


def MSET(o, v):
    return lambda e: e.memset(o, v)


def DMA(o, i):
    return lambda e: e.dma_start(out=o, in_=i)


class Region:
    def __init__(self, ap):
        self.ap = ap
        self.off = 0
        self.cap = ap.shape[1] * 2

    def reset(self):
        self.off = 0

    def take(self, shape, dt):
        esz = 4 if dt == F32 else 2
        n = int(np.prod(shape[1:])) * esz
        assert self.off + n <= self.cap, (self.off, n, self.cap)
        v = self.ap[0:shape[0], self.off // 2:(self.off + n) // 2]
        self.off += (n + 31) // 32 * 32
        if dt == F32:
            v = v.bitcast(F32)
        if len(shape) == 3:
            v = v.rearrange("p (a b) -> p a b", b=shape[2])
        return v


def nsa_blocks():
    bl = []
    HD, GD = H * DH, G * DH
    for i in range(HD // BW):
        bl.append((i * BW, BW, 'B', 'q', 2 * i))
    c = HD
    for name, mode in [('kc', 'B'), ('vc', 'B'), ('ks', 'B'), ('vs', 'A'), ('kw', 'B'), ('vw', 'A')]:
        for i in range(GD // BW):
            bl.append((c + i * BW, BW, mode, name, 2 * i))
        c += GD
    bl.append((c, 96, 'A', 'gates', 0))
    c += 96
    for name in ['zc', 'zs', 'zw']:
        for i in range(HD // BW):
            bl.append((c + i * BW, BW, 'A', name, 2 * i))
        c += HD
    assert c == NSA_IN
    return bl


def sb_blocks(qname, zname):
    bl = []
    HD = H * DH
    for i in range(HD // BW):
        bl.append((i * BW, BW, 'B', qname, 2 * i))
    for i in range(HD // BW):
        bl.append((HD + i * BW, BW, 'A', zname, 2 * i))
    return bl


def out_blocks():
    return [(i * BW, BW, 'O', 'o', i) for i in range(D // BW)]


def tile_weight(w, blocks):
    out = np.zeros((len(blocks), 128, KC * BW), np.float32)
    o4 = out.reshape(len(blocks), 128, KC, BW)
    for bi, (c0, wd, _, _, _) in enumerate(blocks):
        o4[bi, :, :, :wd] = w[:, c0:c0 + wd].reshape(KC, 128, wd).transpose(1, 0, 2)
    return out


def t5_bucket_np(d):
    d = np.maximum(d, 0)
    lr = np.log(np.maximum(d, 16).astype(np.float32) / np.float32(16)).astype(np.float32)
    large = 16 + (lr / np.float32(math.log(128 / 16)) * np.float32(16)).astype(np.int32)
    return np.where(d < 16, d, np.minimum(large, 31))


def host_tables(rel_bias):
    rb = np.asarray(rel_bias, np.float32)
    p = np.arange(128)[:, None]
    def band(X, hi=None):
        x = np.arange(X)[None, :]
        delta = x - 384 - p
        b = rb[t5_bucket_np(delta)]
        ok = delta >= 0
        if hi is not None:
            ok = ok & (delta < hi)
        b = np.where(ok[:, :, None], b, np.float32(NEG))
        return np.ascontiguousarray(b.transpose(2, 0, 1)).astype(np.float32)
    sband = band(SB_X)
    wband = band(WB_X, 512)
    n = np.arange(NCMP)[:, None]
    t = np.arange(T)[None, :]
    dc = t - (16 * n + 31)
    cb = rb[t5_bucket_np(dc)]
    cb = np.where((dc >= 0)[:, :, None], cb, np.float32(NEG))
    cband = np.ascontiguousarray(cb.transpose(2, 0, 1)).astype(np.float32)
    b31 = np.ascontiguousarray(np.broadcast_to(rb[31][None, :], (128, H))).astype(np.float32)
    return sband, wband, cband, b31


def const_tables():
    p = np.arange(128)[:, None]
    x = np.arange(MB_X)[None, :]
    mband = ((x - 384 - p) > 0).astype(np.float32)
    tt = np.arange(NT)[None, :, None]
    pp = np.arange(128)[:, None, None]
    j = np.arange(NSLC)[None, None, :]
    tpos = tt * 128 + pp
    cur = tpos // 64
    forced = (j == 0) | (j == cur) | (j == cur - 1)
    valid = (j * 64) <= tpos
    topA = (valid & ~forced).astype(np.float32)
    topC = np.where(forced, np.float32(1e6), np.where(valid, np.float32(0.0), np.float32(-1.0))).astype(np.float32)
    E = np.zeros((NSLC, NT, 128), np.float32)
    for kt in range(NT):
        E[2 * kt, kt, :64] = 1.0
        E[2 * kt + 1, kt, 64:] = 1.0
    cs = np.arange(NCMP) * 16
    ce = cs + 31
    ss = np.arange(NSLC) * 64
    ov = ((cs[:, None] < ss[None, :] + 64) & (ce[:, None] >= ss[None, :])).astype(np.float32)
    return mband, np.ascontiguousarray(topA), np.ascontiguousarray(topC), E.reshape(NSLC, NT * 128), ov


class Ctx:
    def dump(self, name, ap, reads):
        d = self.nc.dram_tensor("dbg_" + name, list(ap.shape), ap.dtype, kind="ExternalOutput").ap()
        self.P.dma('sp', DMA(d, ap), reads=reads)


def build(dbg=False, stop_after=None, start_at=None, sub=None):
    nc = bass.Bass("TRN2", target_bir_lowering=False)
    okind = "ExternalOutput" if dbg else "Internal"

    def din(name, shape, dt=F32):
        return nc.dram_tensor(name, list(shape), dt, kind="ExternalInput").ap()

    def dscr(name, shape, dt):
        k = okind
        if start_at == 'sb' and name in ('featT', 'szH', 'kshT', 'vshH'):
            k = "ExternalInput"
        elif start_at == 'sb':
            pass
        elif start_at is not None and name in ('featT', 'vsH', 'gatesH', 'szH'):
            k = "ExternalInput"
        return nc.dram_tensor(name, list(shape), dt, kind=k).ap()

    c = Ctx()
    c.nc = nc
    x_in = din("x", [T, D])
    nb_nsa, nb_sb, nb_o = len(nsa_blocks()), len(sb_blocks('q', 'z')), len(out_blocks())
    if start_at is not None:
        nb_nsa = nb_sb = nb_o = 1
    c.sub = sub
    c.dumps = []
    wts = {}
    for l in range(2):
        p = f"a{l}_"
        wts[p + 'w_in'] = din(p + 'w_in', [nb_nsa, 128, KC * BW])
        wts[p + 'w_out'] = din(p + 'w_out', [nb_o, 128, KC * BW])
        wts[p + 'norm'] = din(p + 'norm', [128, KC])
        for kv in 'kv':
            wts[p + f'cmp_{kv}_w1'] = din(p + f'cmp_{kv}_w1', [128, 32 * 128])
            wts[p + f'cmp_{kv}_w2'] = din(p + f'cmp_{kv}_w2', [128, 128])
        wts[p + 'posT'] = din(p + 'posT', [128, 32])
    wts['w_kv'] = din('w_kv', [nb_sb, 128, KC * BW])
    wts['kv_norm'] = din('kv_norm', [128, KC])
    for l in (2, 3):
        p = f"b{l}_"
        wts[p + 'w_in'] = din(p + 'w_in', [nb_sb, 128, KC * BW])
        wts[p + 'w_out'] = din(p + 'w_out', [nb_o, 128, KC * BW])
        wts[p + 'norm'] = din(p + 'norm', [128, KC])
    fin_g = din('final_norm', [1, D])
    sband_d = din('sband', [H, 128, SB_X])
    wband_d = din('wband', [H, 128, WB_X])
    cband_d = din('cband', [H, NCMP, T])
    b31_d = din('b31', [128, H])
    mband_d = din('mband', [128, MB_X])
    topA_d = din('topA', [128, NT * NSLC])
    topC_d = din('topC', [128, NT * NSLC])
    E_d = din('Emat', [NSLC, NT * 128])
    ov_d = din('ovl', [NCMP, NSLC])
    out_d = nc.dram_tensor("out", [T, D], F32, kind="ExternalOutput").ap()

    xres = dscr("xres", [T, D], F32)
    featT = dscr("featT", [48, 128, T], BF16)
    vsH = dscr("vsH", [2, G, 128, NT * DH], BF16)
    gatesH = dscr("gatesH", [128, NT * 96], F32)
    szH = dscr("szH", [3, H, 128, NT * DH], BF16)
    kshT = dscr("kshT", [H, 128, T], BF16)
    vshH = dscr("vshH", [H, 128, NT * DH], BF16)
    mixT = dscr("mixT", [H, 128, T], BF16)
    c.dbg_out = {}

    st = ExitStack()
    with st:
        def sb(name, shape, dt):
            return st.enter_context(nc.sbuf_tensor(name, list(shape), dt))

        def psum(name, shape, dt):
            return st.enter_context(nc.psum_tensor(name, list(shape), dt))

        regA_t = sb("regA", [128, 65536], BF16)
        regW_t = sb("regW", [128, 16384], BF16)
        regS_t = sb("regS", [128, 8192], BF16)
        regC_t = sb("regC", [128, 7168], BF16)
        RA, RW, RS, RC = Region(regA_t[:]), Region(regW_t[:]), Region(regS_t[:]), Region(regC_t[:])
        PS = [psum(f"ps{i}", [128, 512], F32) for i in range(8)]
        PTB = PS[7][:, :].bitcast(BF16)

        identf = RC.take([128, 128], F32)
        identb = RC.take([128, 128], BF16)
        negU = RC.take([128, 128], BF16)
        negones = RC.take([128, 128], BF16)
        epsb = RC.take([128, 1], F32)
        gT = {k: RC.take([128, KC], F32) for k in ['a0_', 'a1_', 'kv_', 'b2_', 'b3_']}
        b31 = RC.take([128, H], F32)
        ss = RC.take([128, NT], F32)
        rs = RC.take([128, NT], F32)
        Emat = RC.take([128, NT, 128], BF16)
        ovl_f = RC.take([NCMP, NSLC], F32)
        onesf = RC.take([128, 128], F32)

        P = Prog(nc, st)
        P.op('pool', MSET(onesf, 1.0), writes=['onesf'])
        P.op('pool', lambda e: e.affine_select(out=identf, in_=onesf, pattern=[[-1, 128]], compare_op=ALU.is_equal,
                                               fill=0.0, base=0, channel_multiplier=1), reads=['onesf'], writes=['identf'])
        P.op('dve', CP(identb, identf), reads=['identf'], writes=['identb'])
        P.op('pool', MSET(negones, -1.0), writes=['negones'])
        P.op('pool', lambda e: e.affine_select(out=negU, in_=negones, pattern=[[-1, 128]], compare_op=ALU.is_ge,
                                               fill=0.0, base=0, channel_multiplier=1), reads=['negones'], writes=['negU'])
        P.op('dve', MSET(epsb, 1e-6), writes=['epsb'])
        for k, nm in [('a0_', 'a0_norm'), ('a1_', 'a1_norm'), ('kv_', 'kv_norm'), ('b2_', 'b2_norm'), ('b3_', 'b3_norm')]:
            P.dma('sp', DMA(gT[k], wts[nm]), writes=[('gT', k)])
        P.dma('sp', DMA(b31, b31_d), writes=['b31'])
        P.op('pool', MSET(Emat.rearrange("p a b -> p (a b)"), 0.0), writes=['Emat'])
        P.dma('pool', DMA(Emat[0:NSLC].rearrange("p a b -> p (a b)"), E_d), writes=['Emat'])
        P.dma('sp', DMA(ovl_f, ov_d), writes=['ovl_f'])

        c.P, c.PS, c.PTB = P, PS, PTB
        c.RA, c.RW, c.RS = RA, RW, RS
        c.identf, c.identb, c.negU, c.negones, c.epsb, c.gT, c.b31 = identf, identb, negU, negones, epsb, gT, b31
        c.ss, c.rs, c.Emat, c.ovl_f = ss, rs, Emat, ovl_f
        c.featT, c.vsH, c.gatesH, c.szH, c.kshT, c.vshH, c.mixT, c.xres = featT, vsH, gatesH, szH, kshT, vshH, mixT, xres
        c.sband_d, c.wband_d, c.cband_d, c.mband_d, c.topA_d, c.topC_d = sband_d, wband_d, cband_d, mband_d, topA_d, topC_d
        c.wts = wts

        def done(tag):
            return stop_after == tag

        finished = False
        xsrc = x_in
        if start_at == 'sb':
            sb_attention(c)
            finished = True
        for l in (range(2) if start_at != 'sb' else ()):
            p = f"a{l}_"
            if start_at is None:
                norm_phase(c, xsrc, gT[p])
            if done(f'n{l}'):
                finished = True
                break
            if start_at is None:
                proj_phase(c, wts[p + 'w_in'], nsa_blocks(), nsa_dest(c))
            if done(f'p{l}'):
                finished = True
                break
            compress_phase(c, p)
            nsa_attention(c, p)
            if done(f'at{l}'):
                finished = True
                break
            out_phase(c, wts[p + 'w_out'], xsrc, xres)
            xsrc = xres
            if done(f'o{l}'):
                finished = True
                break
        if not finished:
            norm_phase(c, xres, gT['kv_'])
            proj_phase(c, wts['w_kv'], sb_blocks('ksh', 'vsh'), sb_dest(c))
            for l in (2, 3):
                p = f"b{l}_"
                norm_phase(c, xres, gT[p])
                proj_phase(c, wts[p + 'w_in'], sb_blocks('q', 'z'), sb_dest(c))
                sb_attention(c)
                out_phase(c, wts[p + 'w_out'], xres, xres)
                if done(f'o{l}'):
                    finished = True
                    break
        if not finished:
            final_norm(c, xres, fin_g, out_d)
        P.barrier()
        P.finish()
        with nc.Block() as block:
            P.emit(block)
    return nc


def norm_phase(c, xsrc, gT):
    P, PS = c.P, c.PS
    P.barrier()
    c.RA.reset(); c.RW.reset(); c.RS.reset()
    hT = c.RA.take([128, KC, T], BF16)
    xt = [c.RW.take([128, D], F32) for _ in range(2)]
    junk = c.RS.take([128, D], F32)
    c.hT = hT
    P.op('dve', MSET(c.ss, 0.0), writes=['ss'])
    for tt in range(NT):
        x_ = xt[tt % 2]
        xk = ('xt', tt % 2)
        P.dma('sp', DMA(x_, xsrc[tt * 128:(tt + 1) * 128, :]), writes=[xk])
        sst, rst = c.ss[:, tt:tt + 1], c.rs[:, tt:tt + 1]
        P.op('act', ACT(junk, x_, AF.Square, accum=sst), reads=[xk, 'ss'], writes=['junk', ('ss', tt)])
        P.op('act', ACT(rst, sst, AF.Sqrt, bias=c.epsb[:, 0:1], scale=1.0 / D), reads=[('ss', tt), 'epsb'], writes=[('rs', tt)])
        P.op('dve', lambda e, o=rst: e.reciprocal(o, o), reads=[('rs', tt)], writes=[('rs', tt)])
        P.op('act', lambda e, o=x_, s=rst: e.mul(o, o, s), reads=[xk, ('rs', tt)], writes=[xk])
        for c4 in range(KC // 4):
            bank = 4 + c4 % 2
            ps = PS[bank]
            for k in range(4):
                kc = c4 * 4 + k
                P.op('pe', TR(ps[:, k * 128:(k + 1) * 128], x_[:, kc * 128:(kc + 1) * 128], c.identf),
                     reads=[xk, 'identf'], writes=[('ps', bank)], inc=(k == 3))
            for k in range(4):
                kc = c4 * 4 + k
                P.op('dve', TS(hT[:, kc, tt * 128:(tt + 1) * 128], ps[:, k * 128:(k + 1) * 128], gT[:, kc:kc + 1], None, ALU.mult),
                     reads=[('ps', bank), 'gTall'], writes=['hT'])


def nsa_dest(c):
    fid = {'q': 0, 'kc': 32, 'vc': 36, 'ks': 40, 'kw': 44}
    zid = {'zc': 0, 'zs': 1, 'zw': 2}

    def dest(kind, idx):
        if kind in fid:
            return c.featT[fid[kind] + idx], (QSCALE if kind == 'q' else None)
        if kind == 'vs':
            return c.vsH[0, idx], AF.Copy
        if kind == 'vw':
            return c.vsH[1, idx], AF.Copy
        if kind in zid:
            return c.szH[zid[kind], idx], AF.Silu
        raise KeyError(kind)
    return dest


def sb_dest(c):
    def dest(kind, idx):
        if kind == 'q':
            return c.featT[idx], QSCALE
        if kind == 'ksh':
            return c.kshT[idx], None
        if kind == 'vsh':
            return c.vshH[idx], AF.Copy
        if kind == 'z':
            return c.szH[0, idx], AF.Silu
        raise KeyError(kind)
    return dest


def proj_phase(c, wT, blocks, dest):
    P, PS = c.P, c.PS
    P.barrier()
    c.RW.reset(); c.RS.reset()
    hT = c.hT
    wbuf = [c.RW.take([128, KC, BW], BF16) for _ in range(2)]
    stage = [c.RS.take([128, 2, T], BF16) for _ in range(2)]
    pc = 0
    ev = 0
    for bi, (c0, wd, mode, kind, idx) in enumerate(blocks):
        wb = wbuf[bi % 2]
        wk = ('wbuf', bi % 2)
        P.dma('pool', DMA(wb.rearrange("p a b -> p (a b)"), wT[bi]), writes=[wk])
        sg = stage[bi % 2]
        sk = ('stage', bi % 2)
        if mode == 'B':
            for sub in range(wd // 128):
                dst, scale = dest(kind, idx + sub)
                for tg in range(4):
                    bank = pc % 4
                    pc += 1
                    ps = PS[bank]
                    for kc in range(KC):
                        P.op('pe', MM(ps[:, :], wb[:, kc, sub * 128:(sub + 1) * 128], hT[:, kc, tg * 512:(tg + 1) * 512], kc == 0, kc == KC - 1),
                             reads=['hT', wk], writes=[('ps', bank)], inc=(kc == KC - 1))
                    o = sg[:, sub, tg * 512:(tg + 1) * 512]
                    if ev % 2 == 0:
                        P.op('act', ACT(o, ps[:, :], AF.Copy, scale=(scale if scale is not None else 1.0)), reads=[('ps', bank)], writes=[sk])
                    else:
                        P.op('dve', TS(o, ps[:, :], (scale if scale is not None else 1.0), None, ALU.mult), reads=[('ps', bank)], writes=[sk])
                    ev += 1
                P.dma('sp', DMA(dst, sg[:, sub, :]), reads=[sk], writes=[('dst', kind, idx + sub)])
        elif kind == 'gates':
            gs = sg.rearrange("p a b -> p (a b)")[:, 0:NT * 96 * 2].bitcast(F32).rearrange("p (t k) -> p t k", k=96)
            for tt in range(NT):
                bank = pc % 4
                pc += 1
                ps = PS[bank]
                for kc in range(KC):
                    P.op('pe', MM(ps[:, 0:96], hT[:, kc, tt * 128:(tt + 1) * 128], wb[:, kc, 0:96], kc == 0, kc == KC - 1),
                         reads=['hT', wk], writes=[('ps', bank)], inc=(kc == KC - 1))
                P.op('act', ACT(gs[:, tt, :], ps[:, 0:96], AF.Sigmoid), reads=[('ps', bank)], writes=[sk])
            P.dma('sp', DMA(c.gatesH, gs.rearrange("p t k -> p (t k)")), reads=[sk], writes=['gatesH'])
        else:
            s4 = sg.rearrange("p h (t d) -> p h t d", d=128)
            func = dest(kind, idx)[1]
            for tt in range(NT):
                bank = pc % 4
                pc += 1
                ps = PS[bank]
                for kc in range(KC):
                    P.op('pe', MM(ps[:, 0:BW], hT[:, kc, tt * 128:(tt + 1) * 128], wb[:, kc, :], kc == 0, kc == KC - 1),
                         reads=['hT', wk], writes=[('ps', bank)], inc=(kc == KC - 1))
                o = s4[:, :, tt, :]
                i = ps[:, 0:BW].rearrange("p (h d) -> p h d", d=128)
                if func == AF.Copy and ev % 2 == 1:
                    P.op('dve', CP(o, i), reads=[('ps', bank)], writes=[sk])
                else:
                    P.op('act', ACT(o, i, func), reads=[('ps', bank)], writes=[sk])
                ev += 1
            for hh in range(2):
                P.dma('sp', DMA(dest(kind, idx + hh)[0], sg[:, hh, :]), reads=[sk], writes=[('dst', kind, idx + hh)])


def compress_phase(c, p):
    pass


def nsa_attention(c, p):
    P, PS, PTB = c.P, c.PS, c.PTB
    RA, RW, RS = c.RA, c.RW, c.RS
    P.barrier()
    RA.reset(); RW.reset(); RS.reset()
    identb, identf, b31 = c.identb, c.identf, c.b31
    kcT = RA.take([128, G, NCMP], BF16)
    vcov = RA.take([NCMP, G, 161], BF16)
    qT = [RA.take([128, T], BF16) for _ in range(2)]
    mixacc = [RA.take([128, NT, DH], F32) for _ in range(2)]
    ksT = RA.take([128, T], BF16)
    kwT = RA.take([128, T], BF16)
    vaug = [RA.take([128, NT, 129], BF16) for _ in range(2)]
    maskT = RA.take([128, T], BF16)
    impacc = RA.take([128, NT, NSLC], F32)
    topA = RA.take([128, NT, NSLC], F32)
    topC = RA.take([128, NT, NSLC], F32)
    gates = RA.take([128, NT, 96], F32)
    sz = [[RA.take([128, NT, DH], BF16) for _ in range(3)] for _ in range(2)]
    sband = [RA.take([128, SB_X], BF16) for _ in range(2)]
    wband = [RA.take([128, WB_X], BF16) for _ in range(2)]
    cband = [RA.take([NCMP, T], BF16) for _ in range(2)]
    pT = [RA.take([128, 512], BF16) for _ in range(6)]
    mixbf = [RA.take([128, DH], BF16) for _ in range(4)]
    mstage = [RA.take([128, T], BF16) for _ in range(2)]
    tmpf = [RA.take([128, DH], F32) for _ in range(4)]
    rzt = RA.take([128, 64], F32)
    wk = RA.take([128, NSLC], F32)
    wk2 = RA.take([128, NSLC], F32)
    mx8 = RA.take([128, 8], F32)
    w1b = RW.take([128, 32, 128], BF16)
    w2b = RW.take([128, 128], BF16)
    posTb = RW.take([128, 32], BF16)
    raw = [RW.take([128, T], BF16) for _ in range(2)]
    s1 = [RW.take([128, NCMP], BF16) for _ in range(2)]
    cpos = RW.take([128, 1], F32)

    P.op('pool', MSET(maskT, 0.0), writes=['maskT'])
    P.dma('sp', DMA(topA.rearrange("p a b -> p (a b)"), c.topA_d), writes=['topA'])
    P.dma('sp', DMA(topC.rearrange("p a b -> p (a b)"), c.topC_d), writes=['topC'])
    P.dma('sp', DMA(gates.rearrange("p a b -> p (a b)"), c.gatesH), writes=['gates'])
    P.dma('pool', DMA(posTb, c.wts[p + 'posT']), writes=['posTb'])
    for i in range(2):
        P.op('pool', MSET(vaug[i][:, :, 128:129], 1.0), writes=[('vaug1', i)])

    rc = 0
    for kv in 'kv':
        P.dma('pool', DMA(w1b.rearrange("p a b -> p (a b)"), c.wts[p + f'cmp_{kv}_w1']), writes=['w1b'])
        P.dma('pool', DMA(w2b, c.wts[p + f'cmp_{kv}_w2']), writes=['w2b'])
        for j in range(32):
            P.op('pe', MM(PS[6][:, 0:1], w1b[:, j, :], posTb[:, j:j + 1], j == 0, j == 31),
                 reads=['w1b', 'posTb'], writes=[('ps', 6)], inc=(j == 31))
        P.op('dve', CP(cpos, PS[6][:, 0:1]), reads=[('ps', 6)], writes=['cpos'])
        for g in range(G):
            rw = raw[rc % 2]
            rk = ('raw', rc % 2)
            s1_ = s1[rc % 2]
            sk = ('s1', rc % 2)
            rc += 1
            P.dma('sp', DMA(rw, c.featT[(32 if kv == 'k' else 36) + g]), writes=[rk])
            rv = rw.rearrange("p (n s) -> p n s", s=16)
            for j in range(32):
                P.op('pe', MM(PS[5][:, 0:NCMP], w1b[:, j, :], rv[:, (j // 16):(j // 16) + NCMP, j % 16], j == 0, j == 31),
                     reads=['w1b', rk], writes=[('ps', 5)], inc=(j == 31))
            P.op('act', ACT(s1_, PS[5][:, 0:NCMP], AF.Silu, bias=cpos[:, 0:1]), reads=[('ps', 5), 'cpos'], writes=[sk])
            if kv == 'k':
                P.op('pe', MM(PS[6][:, 0:NCMP], w2b, s1_, True, True), reads=['w2b', sk], writes=[('ps', 6)])
                P.op('dve', CP(kcT[:, g, :], PS[6][:, 0:NCMP]), reads=[('ps', 6)], writes=[('kcT', g)])
            else:
                P.op('pe', MM(PS[6][0:NCMP, 0:128], s1_, w2b, True, True), reads=['w2b', sk], writes=[('ps', 6)])
                P.op('dve', CP(vcov[:, g, 0:128], PS[6][0:NCMP, 0:128]), reads=[('ps', 6)], writes=[('vcov', g)])
    for g in range(G):
        P.op('pool', MSET(vcov[:, g, 128:129], 1.0), writes=[('vcov1', g)])
        P.op('dve', CP(vcov[:, g, 129:161], c.ovl_f), writes=[('vcovo', g)])

    if c.sub == 'cmp':
        c.dump('kcT', kcT.rearrange("p a b -> p (a b)"), [('kcT', g) for g in range(G)])
        c.dump('vcov', vcov.rearrange("p a b -> p (a b)"), [(n, g) for g in range(G) for n in ('vcov', 'vcov1', 'vcovo')])
        return
    st = {'s': 0, 'p': 0, 'c': 0, 'e': 0, 'i': 0, 'f': 0}

    class Pipe:
        def __init__(self, depth=1):
            self.q = []
            self.depth = depth
            self.deferred = []

        def _tick(self):
            keep = []
            for item in self.deferred:
                item[0] -= 1
                if item[0] <= 0:
                    item[1]()
                else:
                    keep.append(item)
            self.deferred = keep

        def push(self, a, b):
            a()
            self.q.append(b)
            if len(self.q) > self.depth:
                self.q.pop(0)()
            self._tick()

        def defer(self, fn, n):
            self.deferred.append([n, fn])

        def flush(self):
            while self.q:
                self.q.pop(0)()
            for item in self.deferred:
                item[1]()
            self.deferred = []

    def cmp_tile(g, qsl, cb, Q, qk, ck, sbanks=(0, 1)):
        bank = sbanks[st['s'] % len(sbanks)]
        st['s'] += 1
        psS = PS[bank]
        P.op('pe', MM(psS[0:NCMP, :], kcT[:, g, :], qsl, True, False), reads=[('kcT', g), qk], writes=[('ps', bank)], inc=False)
        P.op('pe', MM(psS[0:NCMP, :], identb[0:NCMP, 0:NCMP], cb[:, Q * 512:(Q + 1) * 512], False, True),
             reads=[ck], writes=[('ps', bank)])
        pi = st['p'] % len(pT)
        st['p'] += 1
        pt = pT[pi][0:NCMP, :]
        P.op('act', ACT(pt, psS[0:NCMP, :], AF.Exp), reads=[('ps', bank)], writes=[('pT', pi)])
        return pt, ('pT', pi)

    def pv_banks():
        rr = st['c']
        st['c'] += 1
        return (2, 3) if rr % 2 == 0 else (4, 5)

    def load1(h):
        P.dma('sp', DMA(qT[h % 2], c.featT[h]), writes=[('qT', h % 2)])
        P.dma('pool', DMA(cband[h % 2], c.cband_d[h]), writes=[('cband', h % 2)])

    def load3(h):
        load1(h)
        P.dma('pool', DMA(sband[h % 2], c.sband_d[h]), writes=[('sband', h % 2)])
        P.dma('pool', DMA(wband[h % 2], c.wband_d[h]), writes=[('wband', h % 2)])
        for br in range(3):
            P.dma('sp', DMA(sz[h % 2][br].rearrange("p a b -> p (a b)"), c.szH[br, h]), writes=[('sz', h % 2, br)])

    def evac_round(h, br, banks, Q):
        macc = mixacc[h % 2]
        szh = sz[h % 2]
        for half in range(2):
            b_ = banks[half]
            pok = ('ps', b_)
            tt0 = 4 * Q + 2 * half
            k = 2 * (st['e'] % 8)
            st['e'] += 1
            rz = rzt[:, 32 + k:34 + k]
            rzk = ('rz', k)
            zc = PS[b_][:, :].rearrange("p (s c) -> p s c", c=256)[:, :, 128]
            P.op('dve', TS(rz, zc, 1e-30, None, ALU.max), reads=[pok], writes=[rzk])
            P.op('dve', lambda e, o=rz: e.reciprocal(o, o), reads=[rzk], writes=[rzk])
            P.op('dve', TT(rz, rz, gates[:, tt0:tt0 + 2, br * H + h], ALU.mult), reads=[rzk, 'gates'], writes=[rzk])
            for j in range(2):
                tt = tt0 + j
                po = PS[b_][:, j * 256:j * 256 + 128]
                if br == 0:
                    P.op('dve', STT(macc[:, tt, :], po, rz[:, j:j + 1], szh[br][:, tt, :], ALU.mult, ALU.mult),
                         reads=[pok, rzk, ('sz', h % 2, br)], writes=[('macc', h % 2, tt)])
                else:
                    ti = st['f'] % 4
                    st['f'] += 1
                    tf, tk = tmpf[ti], ('tmpf', ti)
                    P.op('dve', STT(tf, po, rz[:, j:j + 1], szh[br][:, tt, :], ALU.mult, ALU.mult),
                         reads=[pok, rzk, ('sz', h % 2, br)], writes=[tk])
                    if br == 1:
                        P.op('pool', TT(macc[:, tt, :], macc[:, tt, :], tf, ALU.add), reads=[tk, ('macc', h % 2, tt)], writes=[('macc', h % 2, tt)])
                    else:
                        mb, mbk = mixbf[tt % 4], ('mixbf', tt % 4)
                        P.op('pool', TT(mb, macc[:, tt, :], tf, ALU.add), reads=[tk, ('macc', h % 2, tt)], writes=[mbk])

    def finalize(h, Q):
        mst, mstk = mstage[h % 2], ('mstage', h % 2)
        for qs in range(4):
            tt = Q * 4 + qs
            mb, mbk = mixbf[tt % 4], ('mixbf', tt % 4)
            P.op('pe', TR(PTB[:, qs * 128:(qs + 1) * 128], mb, identb), reads=[mbk], writes=[('ps', 7)])
        P.op('dve', CP(mst[:, Q * 512:(Q + 1) * 512], PTB[:, 0:512]), reads=[('ps', 7)], writes=[mstk])
        if Q == NQ - 1:
            P.dma('sp', DMA(c.mixT[h], mst), reads=[mstk], writes=[('mixT', h)])

    for g in range(G):
        P.dma('sp', DMA(ksT, c.featT[40 + g]), writes=['ksT'])
        P.dma('sp', DMA(kwT, c.featT[44 + g]), writes=['kwT'])
        for i in range(2):
            P.dma('sp', DMA(vaug[i][:, :, 0:128], c.vsH[i, g].rearrange("p (t d) -> p t d", d=128)), writes=[('vaug', i)])
        pipe = Pipe()
        load1(g * R)
        for r in range(R):
            h = g * R + r
            if r + 1 < R:
                load1(h + 1)
            for Q in range(NQ):
                hold = {}

                def A(h=h, Q=Q, hold=hold):
                    hold['pt'], hold['pk'] = cmp_tile(g, qT[h % 2][:, Q * 512:(Q + 1) * 512], cband[h % 2], Q, ('qT', h % 2), ('cband', h % 2))

                def B(h=h, Q=Q, r=r, hold=hold):
                    pt, pk = hold['pt'], hold['pk']
                    bank = 6 + st['i'] % 2
                    st['i'] += 1
                    bkey = ('ps', bank)
                    for qs in range(4):
                        po = PS[bank][:, qs * 64:qs * 64 + 33]
                        P.op('pe', MM(po, pt[:, qs * 128:(qs + 1) * 128], vcov[:, g, 128:161], qs == 0, True, skip=True),
                             reads=[pk, ('vcov1', g), ('vcovo', g)], writes=[bkey], inc=(qs == 3))
                    for qs in range(4):
                        tt = Q * 4 + qs
                        po = PS[bank][:, qs * 64:qs * 64 + 33]
                        rz = rzt[:, 16 + 4 * (st['i'] % 2) + qs:17 + 4 * (st['i'] % 2) + qs]
                        rzk = ('rz1', st['i'] % 2, qs)
                        P.op('dve', TS(rz, po[:, 0:1], 1e-30, None, ALU.max), reads=[bkey], writes=[rzk])
                        P.op('dve', lambda e, o=rz: e.reciprocal(o, o), reads=[rzk], writes=[rzk])
                        if r == 0:
                            P.op('dve', TS(impacc[:, tt, :], po[:, 1:33], rz, None, ALU.mult), reads=[bkey, rzk], writes=[('imp', tt)])
                        else:
                            P.op('dve', STT(impacc[:, tt, :], po[:, 1:33], rz, impacc[:, tt, :], ALU.mult, ALU.add),
                                 reads=[bkey, rzk, ('imp', tt)], writes=[('imp', tt)])
                pipe.push(A, B)
        pipe.flush()
        if c.sub == 's1':
            c.dump('impacc', impacc.rearrange("p a b -> p (a b)"), [('imp', tt) for tt in range(NT)])
            return
        load3(g * R)
        for tt in range(NT):
            P.op('dve', TT(wk, impacc[:, tt, :], topA[:, tt, :], ALU.mult), reads=[('imp', tt), 'topA'], writes=['wk'])
            P.op('dve', TT(wk, wk, topC[:, tt, :], ALU.add), reads=['wk', 'topC'], writes=['wk'])
            P.op('dve', lambda e: e.max(out=mx8, in_=wk), reads=['wk'], writes=['mx8'])
            P.op('dve', lambda e: e.match_replace(out=wk2, in_to_replace=mx8, in_values=wk, imm_value=-1e9), reads=['wk', 'mx8'], writes=['wk2'])
            P.op('dve', lambda e: e.max(out=mx8, in_=wk2), reads=['wk2'], writes=['mx8'])
            P.op('dve', lambda e: e.match_replace(out=wk2, in_to_replace=mx8, in_values=wk2, imm_value=-1e9), reads=['wk2', 'mx8'], writes=['wk2'])
            P.op('dve', TS(wk, wk2, -1e8, None, ALU.is_le), reads=['wk2'], writes=['wk'])
            P.op('dve', TS(wk, wk, -1.0, -NEG, ALU.add, ALU.mult), reads=['wk'], writes=['wk'])
            P.op('pe', TR(PS[6][0:NSLC, 0:128], wk, identf), reads=['wk'], writes=[('ps', 6)])
            P.op('dve', CP(maskT[0:NSLC, tt * 128:(tt + 1) * 128], PS[6][0:NSLC, 0:128]), reads=[('ps', 6)], writes=['maskT'])
        if c.sub == 's2':
            c.dump('maskT', maskT[0:NSLC, :], ['maskT'])
            return
        pipe = Pipe(2)
        SB3 = (0, 1, 6)
        for r in range(R):
            h = g * R + r
            tile_no = 0
            for Q in range(NQ):
                rounds = [(0, [0])]
                rounds.append((1, list(range(0, 4 * Q + 4))))
                rounds.append((2, list(range(max(0, 4 * Q - 4), 4 * Q + 4))))
                for br, kts in rounds:
                    rctx = {}
                    for ki, kt in enumerate(kts):
                        hold = {}
                        is_first = (ki == 0)
                        is_last = (ki == len(kts) - 1)
                        pre = (tile_no == 2) and r + 1 < R
                        tile_no += 1

                        def A(h=h, Q=Q, br=br, kt=kt, hold=hold, pre=pre):
                            if pre:
                                load3(h + 1)
                            q_, qk = qT[h % 2], ('qT', h % 2)
                            qsl = q_[:, Q * 512:(Q + 1) * 512]
                            if br == 0:
                                hold['pt'], hold['pk'] = cmp_tile(g, qsl, cband[h % 2], Q, qk, ('cband', h % 2), SB3)
                                return
                            kT_, kk = (ksT, 'ksT') if br == 1 else (kwT, 'kwT')
                            bd, bk = (sband[h % 2], ('sband', h % 2)) if br == 1 else (wband[h % 2], ('wband', h % 2))
                            Dd = Q * 512 - kt * 128
                            far = (br == 1 and Dd >= 256)
                            bank = SB3[st['s'] % 3]
                            st['s'] += 1
                            psS = PS[bank]
                            P.op('pe', MM(psS[:, :], kT_[:, kt * 128:(kt + 1) * 128], qsl, True, False), reads=[kk, qk], writes=[('ps', bank)], inc=False)
                            if br == 1 and os.environ.get('K_EXP_NOE'):
                                if far:
                                    P.op('pe', MM(psS[:, :], identb, bd[:, 0:512], False, True), reads=[bk], writes=[('ps', bank)])
                            elif br == 1:
                                P.op('pe', MM(psS[:, :], c.Emat[:, kt, :], maskT[:, Q * 512:(Q + 1) * 512], False, far),
                                     reads=['maskT'], writes=[('ps', bank)], inc=far)
                            if not far:
                                x0 = Dd + 384
                                P.op('pe', MM(psS[:, :], identb, bd[:, x0:x0 + 512], False, True), reads=[bk], writes=[('ps', bank)])
                            pi = st['p'] % len(pT)
                            st['p'] += 1
                            P.op('act', ACT(pT[pi], psS[:, :], AF.Exp, bias=(b31[:, h:h + 1] if far else None)), reads=[('ps', bank)], writes=[('pT', pi)])
                            hold['pt'], hold['pk'] = pT[pi], ('pT', pi)

                        def B(h=h, Q=Q, br=br, kt=kt, hold=hold, rctx=rctx, is_first=is_first, is_last=is_last):
                            if is_first:
                                rctx['banks'] = pv_banks()
                            banks = rctx['banks']

                            def slot(qs):
                                b_ = banks[qs // 2]
                                return PS[b_][:, (qs % 2) * 256:(qs % 2) * 256 + 129], ('ps', b_)
                            pt, pk = hold['pt'], hold['pk']
                            if br == 0:
                                for qs in range(4):
                                    po, pok = slot(qs)
                                    P.op('pe', MM(po, pt[:, qs * 128:(qs + 1) * 128], vcov[:, g, 0:129], qs % 2 == 0, True, skip=True),
                                         reads=[pk, ('vcov', g), ('vcov1', g)], writes=[pok], inc=(qs % 2 == 1))
                            else:
                                vg = br - 1
                                for qs in range(4):
                                    tq = 4 * Q + qs
                                    lo = 0 if br == 1 else max(0, tq - 4)
                                    if kt < lo or kt > tq:
                                        continue
                                    po, pok = slot(qs)
                                    if os.environ.get('K_EXP_N128'):
                                        po = po[:, 0:128]
                                    P.op('pe', MM(po, pt[:, qs * 128:(qs + 1) * 128], (vaug[vg][:, kt, 0:128] if os.environ.get('K_EXP_N128') else vaug[vg][:, kt, :]), (kt == lo and qs % 2 == 0), kt == tq, skip=True),
                                         reads=[pk, ('vaug', vg), ('vaug1', vg)], writes=[pok], inc=(kt == tq))
                            if is_last:
                                evac_round(h, br, banks, Q)
                                if br == 2:
                                    pipe.defer(lambda h=h, Q=Q: finalize(h, Q), 3)
                        pipe.push(A, B)
            if c.sub and c.sub.startswith('s3'):
                pipe.flush()
                return
        pipe.flush()


def out_phase(c, wT, xsrc, xdst):
    P, PS = c.P, c.PS
    P.barrier()
    c.RA.reset(); c.RW.reset(); c.RS.reset()
    mT = c.RA.take([128, KC, T], BF16)
    wbuf = [c.RW.take([128, KC, BW], BF16) for _ in range(2)]
    ost = c.RS.take([128, NT, BW], F32)
    for h in range(H):
        P.dma('sp', DMA(mT[:, h, :], c.mixT[h]), writes=[('mT', h)])
    xs3 = xsrc.rearrange("(t p) c -> p t c", p=128)
    xd3 = xdst.rearrange("(t p) c -> p t c", p=128)
    pc = 0
    for bi in range(D // BW):
        wb, wk = wbuf[bi % 2], ('wbuf', bi % 2)
        P.dma('pool', DMA(wb.rearrange("p a b -> p (a b)"), wT[bi]), writes=[wk])
        P.dma('sp', DMA(ost, xs3[:, :, bi * BW:(bi + 1) * BW]), reads=[('xcol', bi)], writes=['ost'])
        for tt in range(NT):
            bank = pc % 4
            pc += 1
            ps = PS[bank]
            for kc in range(KC):
                P.op('pe', MM(ps[:, 0:BW], mT[:, kc, tt * 128:(tt + 1) * 128], wb[:, kc, :], kc == 0, kc == KC - 1),
                     reads=[('mT', kc), wk], writes=[('ps', bank)], inc=(kc == KC - 1))
            P.op('dve', TT(ost[:, tt, :], ps[:, 0:BW], ost[:, tt, :], ALU.add), reads=[('ps', bank), 'ost'], writes=['ost'])
        P.dma('sp', DMA(xd3[:, :, bi * BW:(bi + 1) * BW], ost), reads=['ost'], writes=[('xcol', bi)])


def sb_attention(c):
    P, PS, PTB = c.P, c.PS, c.PTB
    RA, RW, RS = c.RA, c.RW, c.RS
    P.barrier()
    RA.reset(); RW.reset(); RS.reset()
    identb, negU, negones = c.identb, c.negU, c.negones
    qT = [RA.take([128, T], BF16) for _ in range(2)]
    kT = [RA.take([128, T], BF16) for _ in range(2)]
    vh = [RA.take([128, NT, DH], BF16) for _ in range(2)]
    szh = [RA.take([128, NT, DH], BF16) for _ in range(2)]
    mband = RA.take([128, MB_X], BF16)
    e_f = [RA.take([128, 512], F32) for _ in range(2)]
    sp_b = [RA.take([128, 512], BF16) for _ in range(3)]
    spm_b = [RA.take([128, 512], BF16) for _ in range(3)]
    sacc_f = RA.take([128, 512], F32)
    sacc_b = [RA.take([128, 512], BF16) for _ in range(3)]
    a_b = [RA.take([128, 512], BF16) for _ in range(4)]
    mixbf = [RA.take([128, DH], BF16) for _ in range(4)]
    mstage = [RA.take([128, T], BF16) for _ in range(2)]
    P.dma('pool', DMA(mband, c.mband_d), writes=['mband'])
    P.op('dve', TS(mband, mband, -1.0, -NEG, ALU.add, ALU.mult), reads=['mband'], writes=['mband'])
    st = {'t': 0, 'a': 0, 'w': 0, 'r': 0}

    def loads(h):
        P.dma('sp', DMA(qT[h % 2], c.featT[h]), writes=[('qT', h % 2)])
        P.dma('sp', DMA(kT[h % 2], c.kshT[h]), writes=[('kT', h % 2)])
        P.dma('sp', DMA(vh[h % 2].rearrange("p a b -> p (a b)"), c.vshH[h]), writes=[('vh', h % 2)])
        P.dma('sp', DMA(szh[h % 2].rearrange("p a b -> p (a b)"), c.szH[0, h]), writes=[('szh', h % 2)])

    q3 = [None, None]
    deferred = []

    def tick():
        keep = []
        for item in deferred:
            item[0] -= 1
            if item[0] <= 0:
                item[1]()
            else:
                keep.append(item)
        deferred[:] = keep

    def push(A, B, C):
        A()
        if q3[0] is not None:
            q3[0][0]()
        if q3[1] is not None:
            q3[1][1]()
        q3[1] = q3[0]
        q3[0] = (B, C)
        tick()

    def flush():
        if q3[0] is not None:
            q3[0][0]()
        if q3[1] is not None:
            q3[1][1]()
        if q3[0] is not None:
            q3[0][1]()
        q3[0] = q3[1] = None
        for item in deferred:
            item[1]()
        deferred[:] = []

    loads(0)
    for h in range(H):
        q_, qk = qT[h % 2], ('qT', h % 2)
        k_, kk = kT[h % 2], ('kT', h % 2)
        tile_no = 0
        for Q in range(NQ):
            kts = list(range(4 * Q + 3, -1, -1))
            n = len(kts)
            rc = {}
            for i, kt in enumerate(kts):
                hold = {}
                pre = (tile_no == 6) and h + 1 < H
                tile_no += 1

                def A(h=h, Q=Q, i=i, kt=kt, n=n, hold=hold, rc=rc, pre=pre, q_=q_, qk=qk, k_=k_, kk=kk):
                    if pre:
                        loads(h + 1)
                    Dd = Q * 512 - kt * 128
                    diag = Dd < 128
                    c0 = max(0, -Dd)
                    qsl = q_[:, Q * 512 + c0:(Q + 1) * 512]
                    ksl = k_[:, kt * 128:(kt + 1) * 128]
                    t = st['t']
                    st['t'] += 1
                    b0 = t % 2
                    psL = PS[b0]
                    if diag:
                        x0 = Dd + 384
                        P.op('pe', MM(psL[:, c0:], ksl, qsl, True, False), reads=[kk, qk], writes=[('ps', b0)], inc=False)
                        P.op('pe', MM(psL[:, c0:], identb, mband[:, x0 + c0:x0 + 512], False, True), reads=['mband'], writes=[('ps', b0)])
                    else:
                        P.op('pe', MM(psL[:, c0:], ksl, qsl, True, True), reads=[kk, qk], writes=[('ps', b0)])
                    ef, efk = e_f[b0], ('e_f', b0)
                    P.op('act', ACT(ef[:, c0:], psL[:, c0:], AF.Exp), reads=[('ps', b0)], writes=[efk])
                    w3 = t % 3
                    spb, spk = sp_b[w3], ('sp_b', w3)
                    P.op('act', ACT(spb[:, c0:], ef[:, c0:], AF.Ln, bias=1.0), reads=[efk], writes=[spk])
                    spm, smk = spb, spk
                    hold.update(spm=spm, smk=smk, diag=diag, Dd=Dd, qsl=qsl, ksl=ksl, c0=c0)
                    if i == 0:
                        rc['sacc'] = (spm, smk)
                        rc['pc'] = c0
                        if n > 2:
                            P.op('dve', CP(sacc_f[:, c0:], spm[:, c0:]), reads=[smk], writes=['sacc_f'])
                    else:
                        pc = rc['pc']
                        hold['sacc_prev'] = rc['sacc']
                        hold['pc'] = pc
                        if i + 1 < n:
                            wb = st['w'] % 3
                            st['w'] += 1
                            P.op('dve', TT(sacc_b[wb][:, pc:], sacc_f[:, pc:], spm[:, pc:], ALU.add), reads=['sacc_f', smk], writes=[('sacc_b', wb)])
                            if c0 < pc:
                                P.op('dve', CP(sacc_b[wb][:, c0:pc], spm[:, c0:pc]), reads=[smk], writes=[('sacc_b', wb)])
                            rc['sacc'] = (sacc_b[wb], ('sacc_b', wb))
                        if i + 2 < n:
                            P.op('dve', TT(sacc_f[:, pc:], sacc_f[:, pc:], spm[:, pc:], ALU.add), reads=['sacc_f', smk], writes=['sacc_f'])
                            if c0 < pc:
                                P.op('dve', CP(sacc_f[:, c0:pc], spm[:, c0:pc]), reads=[smk], writes=['sacc_f'])
                        rc['pc'] = c0

                def B(i=i, hold=hold, qk=qk, kk=kk):
                    t = st['a']
                    st['a'] += 1
                    b0 = 2 + t % 2
                    psB = PS[b0]
                    spm, smk, c0 = hold['spm'], hold['smk'], hold['c0']
                    P.op('pe', MM(psB[:, c0:], hold['ksl'], hold['qsl'], True, False, skip=True), reads=[kk, qk], writes=[('ps', b0)], inc=False)
                    if hold['diag']:
                        x0 = hold['Dd'] + 384
                        P.op('pe', MM(psB[:, c0:], identb, mband[:, x0 + c0:x0 + 512], False, False, skip=True), reads=['mband'], writes=[('ps', b0)], inc=False)
                    P.op('pe', MM(psB[:, c0:], negU, spm[:, c0:], False, i == 0, skip=True), reads=[smk], writes=[('ps', b0)], inc=(i == 0))
                    if i > 0:
                        sa, sak = hold['sacc_prev']
                        pc = hold['pc']
                        P.op('pe', MM(psB[:, pc:], negones, sa[:, pc:], False, True, skip=True), reads=[sak], writes=[('ps', b0)])
                    w4 = t % 4
                    ab, abk = a_b[w4], ('a_b', w4)
                    P.op('act', ACT(ab[:, c0:], psB[:, c0:], AF.Exp), reads=[('ps', b0)], writes=[abk])
                    hold.update(ab=ab, abk=abk)

                def C(h=h, Q=Q, i=i, kt=kt, n=n, hold=hold, rc=rc):
                    if i == 0:
                        rc['pb'] = 4 + st['r'] % 2
                        st['r'] += 1
                    pb = rc['pb']
                    pso = PS[pb]
                    v_, vk = vh[h % 2], ('vh', h % 2)
                    ab, abk = hold['ab'], hold['abk']
                    for qs in range(4):
                        tq = 4 * Q + qs
                        if kt > tq:
                            continue
                        po = pso[:, qs * 128:(qs + 1) * 128]
                        P.op('pe', MM(po, ab[:, qs * 128:(qs + 1) * 128], v_[:, kt, :], (i == 0 and qs == 3), kt == 0, skip=True),
                             reads=[abk, vk], writes=[('ps', pb)], inc=(kt == 0))
                    if i == n - 1:
                        def epilogue(h=h, Q=Q, pb=pb):
                            pso_ = PS[pb]
                            z_, zk = szh[h % 2], ('szh', h % 2)
                            mst, mstk = mstage[h % 2], ('mstage', h % 2)
                            for qs in range(4):
                                tt = 4 * Q + qs
                                mb, mbk = mixbf[qs], ('mixbf', qs)
                                P.op('dve', TT(mb, pso_[:, qs * 128:(qs + 1) * 128], z_[:, tt, :], ALU.mult), reads=[('ps', pb), zk], writes=[mbk])
                                P.op('pe', TR(PTB[:, qs * 128:(qs + 1) * 128], mb, identb), reads=[mbk], writes=[('ps', 7)])
                            P.op('dve', CP(mst[:, Q * 512:(Q + 1) * 512], PTB[:, 0:512]), reads=[('ps', 7)], writes=[mstk])
                            if Q == NQ - 1:
                                P.dma('sp', DMA(c.mixT[h], mst), reads=[mstk], writes=[('mixT', h)])
                        deferred.append([3, epilogue])
                push(A, B, C)
    flush()


def final_norm(c, xsrc, fin_g, out_d):
    P = c.P
    P.barrier()
    c.RA.reset()
    gB = c.RA.take([128, D], F32)
    xt = [c.RA.take([128, D], F32) for _ in range(2)]
    junk = c.RA.take([128, D], F32)
    P.dma('sp', DMA(gB, fin_g.partition_broadcast(128)), writes=['gB'])
    P.op('dve', MSET(c.ss, 0.0), writes=['ss'])
    for tt in range(NT):
        x_, xk = xt[tt % 2], ('xt', tt % 2)
        P.dma('sp', DMA(x_, xsrc[tt * 128:(tt + 1) * 128, :]), writes=[xk])
        sst, rst = c.ss[:, tt:tt + 1], c.rs[:, tt:tt + 1]
        P.op('act', ACT(junk, x_, AF.Square, accum=sst), reads=[xk, 'ss'], writes=['junk', ('ss', tt)])
        P.op('act', ACT(rst, sst, AF.Sqrt, bias=c.epsb[:, 0:1], scale=1.0 / D), reads=[('ss', tt)], writes=[('rs', tt)])
        P.op('dve', lambda e, o=rst: e.reciprocal(o, o), reads=[('rs', tt)], writes=[('rs', tt)])
        P.op('dve', STT(x_, x_, rst, gB, ALU.mult, ALU.mult), reads=[xk, ('rs', tt), 'gB'], writes=[xk])
        P.dma('sp', DMA(out_d[tt * 128:(tt + 1) * 128, :], x_), reads=[xk], writes=[('out', tt)])


def prep_shared(inp):
    sh = {}
    nb, sbq, ob = nsa_blocks(), sb_blocks('q', 'z'), out_blocks()

    def gT(v):
        return np.ascontiguousarray(np.asarray(v, np.float32).reshape(KC, 128).T)
    for l in range(2):
        p = f"a{l}_"
        sh[p + 'w_in'] = tile_weight(np.asarray(inp[p + 'w_in'], np.float32), nb)
        sh[p + 'w_out'] = tile_weight(np.asarray(inp[p + 'w_out'], np.float32), ob)
        sh[p + 'norm'] = gT(inp[p + 'norm'])
        for kv in 'kv':
            w1 = np.asarray(inp[p + f'cmp_{kv}_w1'], np.float32)
            sh[p + f'cmp_{kv}_w1'] = np.ascontiguousarray(w1.reshape(32, 128, 128).transpose(1, 0, 2).reshape(128, 32 * 128))
            sh[p + f'cmp_{kv}_w2'] = np.ascontiguousarray(np.asarray(inp[p + f'cmp_{kv}_w2'], np.float32))
        sh[p + 'posT'] = np.ascontiguousarray(np.asarray(inp[p + 'cmp_pos'], np.float32).T)
    sh['w_kv'] = tile_weight(np.asarray(inp['w_kv'], np.float32), sbq)
    sh['kv_norm'] = gT(inp['kv_norm'])
    for l in (2, 3):
        p = f"b{l}_"
        sh[p + 'w_in'] = tile_weight(np.asarray(inp[p + 'w_in'], np.float32), sbq)
        sh[p + 'w_out'] = tile_weight(np.asarray(inp[p + 'w_out'], np.float32), ob)
        sh[p + 'norm'] = gT(inp[p + 'norm'])
    sh['final_norm'] = np.ascontiguousarray(np.asarray(inp['final_norm'], np.float32).reshape(1, D))
    sband, wband, cband, b31 = host_tables(inp['rel_bias'])
    sh['sband'], sh['wband'], sh['cband'], sh['b31'] = sband, wband, cband, b31
    mband, topA, topC, E, ov = const_tables()
    sh['mband'] = mband
    sh['topA'] = topA.reshape(128, NT * NSLC)
    sh['topC'] = topC.reshape(128, NT * NSLC)
    sh['Emat'] = E
    sh['ovl'] = ov
    return sh


def kernel(**inputs):
    n = 8
    x = np.asarray(inputs['x'], np.float32)
    sh = prep_shared(inputs)
    nc = build()
    in_maps = []
    for b in range(n):
        m = dict(sh)
        m['x'] = np.ascontiguousarray(x[b])
        in_maps.append(m)
    res = run_bass_kernel_spmd(nc, in_maps, core_ids=list(range(n)))
    return np.stack([r['out'] for r in res.results], axis=0).astype(np.float32)
```
